# Optimizing a Trainium2 kernel written in Bass

```python
import jax, jax.numpy as jnp
from jax import lax
import numpy as np


D_MODEL = 1024
BATCH = 8
SEQ = 2048
DEPTH = 4
DEC_BATCH = 128
DEC_SEQ = 4
PAST_LEN = 16384
PAGE_SIZE = 128

D_A = D_MODEL // 4
D_B = 3 * D_MODEL // 8
D_C = D_MODEL - D_A - D_B
POOL_WINDOWS = (2, 4, 8, 16)
POOL_MAX = 16
POOL_GC = D_A // len(POOL_WINDOWS)
SGU_HEADS = 4
SGU_HD = D_B // SGU_HEADS
SGU_CHUNK = 128
CONV_K = 31
FFN_K = 3
D_FF = ((8 * D_MODEL // 3 + 127) // 128) * 128
D_IN = D_A + 2 * D_B + 2 * D_C
EPS = 1e-6

kernel_name = "hybrid_pool_sgu_conformer_decoder_step"


def rmsnorm(x, g):
    xf = x.astype(jnp.float32)
    r = lax.rsqrt(jnp.mean(xf * xf, axis=-1, keepdims=True) + EPS)
    return (xf * r).astype(x.dtype) * g


def layernorm(x, g, b):
    xf = x.astype(jnp.float32)
    mu = jnp.mean(xf, axis=-1, keepdims=True)
    var = jnp.mean(jnp.square(xf - mu), axis=-1, keepdims=True)
    return ((xf - mu) * lax.rsqrt(var + EPS)).astype(x.dtype) * g + b


def pool_mix(xa, prev, pos0, w_pool, scale):
    B, T, _ = xa.shape
    P = POOL_MAX - 1
    xp = jnp.concatenate([prev, xa], axis=1).astype(jnp.float32)
    cs = jnp.concatenate([jnp.zeros((B, 1, D_A), jnp.float32), jnp.cumsum(xp, axis=1)], axis=1)
    pos = pos0 + jnp.arange(T)
    outs = []
    for g, w in enumerate(POOL_WINDOWS):
        sl = slice(g * POOL_GC, (g + 1) * POOL_GC)
        s = cs[:, P + 1:P + 1 + T, sl] - cs[:, P + 1 - w:P + 1 - w + T, sl]
        cnt = jnp.minimum(pos + 1, w).astype(jnp.float32)
        outs.append(s / cnt[None, :, None])
    pooled = jnp.concatenate(outs, axis=-1) - xp[:, P:]
    pooled = pooled.reshape(B, T, len(POOL_WINDOWS), POOL_GC)
    mixed = jnp.einsum('btgc,gcd->btgd', pooled, w_pool.astype(jnp.float32)).reshape(B, T, D_A)
    out = (mixed * scale.astype(jnp.float32)).astype(xa.dtype)
    return out, xp[:, -P:].astype(xa.dtype)


def spatial_gate(u, v, w_s, b_s):
    B, T, _ = v.shape
    L = min(T, SGU_CHUNK)
    n = T // L
    mask = jnp.tril(jnp.ones((L, L), dtype=bool))
    w = jnp.where(mask[None], w_s[:, :L, :L], 0)
    vh = v.reshape(B, n, L, SGU_HEADS, SGU_HD)
    s = jnp.einsum('hts,bnshc->bnthc', w, vh) + b_s[:, :L].T[None, None, :, :, None]
    return u * s.reshape(B, T, D_B)


def causal_dwconv(x, prev, w, b):
    K, C = w.shape
    xp = jnp.concatenate([prev, x], axis=1)
    y = lax.conv_general_dilated(xp, w[:, None, :], window_strides=(1,), padding='VALID',
                                 dimension_numbers=('NWC', 'WIO', 'NWC'),
                                 feature_group_count=C) + b
    return y, xp[:, -(K - 1):]


def trunk(x, pool_prev, conv_prev, ffn_prev, pos0, p):
    pool_new, conv_new, ffn_new, v_rows = [], [], [], []
    for l in range(DEPTH):
        h = rmsnorm(x, p['norm1'][l])
        z = h @ p['w_in'][l]
        za = z[..., :D_A]
        zu = z[..., D_A:D_A + D_B]
        zv = z[..., D_A + D_B:D_A + 2 * D_B]
        zc = z[..., D_A + 2 * D_B:]
        a_out, pst = pool_mix(za, pool_prev[l], pos0, p['pool_w'][l], p['pool_scale'][l])
        b_out = spatial_gate(zu, zv, p['sgu_w'][l], p['sgu_b'][l])
        c_in = zc[..., :D_C] * jax.nn.sigmoid(zc[..., D_C:])
        c_conv, cst = causal_dwconv(c_in, conv_prev[l], p['conv_w'][l], p['conv_b'][l])
        c_out = jax.nn.silu(layernorm(c_conv, p['cnorm_g'][l], p['cnorm_b'][l]))
        x = x + jnp.concatenate([a_out, b_out, c_out], axis=-1) @ p['w_out'][l]
        h2 = rmsnorm(x, p['norm2'][l])
        up = h2 @ p['w_up'][l]
        gate, val = up[..., :D_FF], up[..., D_FF:]
        gate_c, fst = causal_dwconv(gate, ffn_prev[l], p['ffn_conv_w'][l], p['ffn_conv_b'][l])
        x = x + (jax.nn.silu(gate_c) * val) @ p['w_down'][l]
        pool_new.append(pst)
        conv_new.append(cst)
        ffn_new.append(fst)
        v_rows.append(zv)
    y = rmsnorm(x, p['final_norm'])
    return y, jnp.stack(pool_new), jnp.stack(conv_new), jnp.stack(ffn_new), jnp.stack(v_rows)


def setup_inputs(seed: int = 0) -> dict:
    key = jax.random.key(seed)
    ks = jax.random.split(key, 24)
    f32 = jnp.float32
    nrm = lambda k, s, sc: (jax.random.normal(k, s, f32) * sc)
    return {
        'x_prompt': nrm(ks[0], (BATCH, SEQ, D_MODEL), 1.0),
        'x_sample': nrm(ks[1], (DEC_BATCH, DEC_SEQ, D_MODEL), 1.0),
        'state_pool': nrm(ks[2], (DEPTH, DEC_BATCH, POOL_MAX - 1, D_A), 1.0),
        'state_conv': nrm(ks[3], (DEPTH, DEC_BATCH, CONV_K - 1, D_C), 0.5),
        'state_ffn_conv': nrm(ks[4], (DEPTH, DEC_BATCH, FFN_K - 1, D_FF), 1.0),
        'norm1': 1.0 + nrm(ks[5], (DEPTH, D_MODEL), 0.02),
        'w_in': nrm(ks[6], (DEPTH, D_MODEL, D_IN), D_MODEL ** -0.5),
        'pool_w': nrm(ks[7], (DEPTH, len(POOL_WINDOWS), POOL_GC, POOL_GC), POOL_GC ** -0.5),
        'pool_scale': 1.0 + nrm(ks[8], (DEPTH, D_A), 0.1),
        'sgu_w': nrm(ks[9], (DEPTH, SGU_HEADS, SGU_CHUNK, SGU_CHUNK), SGU_CHUNK ** -0.5),
        'sgu_b': 1.0 + nrm(ks[10], (DEPTH, SGU_HEADS, SGU_CHUNK), 0.1),
        'conv_w': nrm(ks[11], (DEPTH, CONV_K, D_C), CONV_K ** -0.5),
        'conv_b': nrm(ks[12], (DEPTH, D_C), 0.02),
        'cnorm_g': 1.0 + nrm(ks[13], (DEPTH, D_C), 0.02),
        'cnorm_b': nrm(ks[14], (DEPTH, D_C), 0.02),
        'w_out': nrm(ks[15], (DEPTH, D_MODEL, D_MODEL), D_MODEL ** -0.5),
        'norm2': 1.0 + nrm(ks[16], (DEPTH, D_MODEL), 0.02),
        'w_up': nrm(ks[17], (DEPTH, D_MODEL, 2 * D_FF), D_MODEL ** -0.5),
        'ffn_conv_w': nrm(ks[18], (DEPTH, FFN_K, D_FF), FFN_K ** -0.5),
        'ffn_conv_b': nrm(ks[19], (DEPTH, D_FF), 0.02),
        'w_down': nrm(ks[20], (DEPTH, D_FF, D_MODEL), D_FF ** -0.5),
        'final_norm': 1.0 + nrm(ks[21], (D_MODEL,), 0.02),
    }


def reference(x_prompt, x_sample, state_pool, state_conv, state_ffn_conv,
              norm1, w_in, pool_w, pool_scale, sgu_w, sgu_b, conv_w, conv_b,
              cnorm_g, cnorm_b, w_out, norm2, w_up, ffn_conv_w, ffn_conv_b,
              w_down, final_norm):
    p = dict(norm1=norm1, w_in=w_in, pool_w=pool_w, pool_scale=pool_scale,
             sgu_w=sgu_w, sgu_b=sgu_b, conv_w=conv_w, conv_b=conv_b,
             cnorm_g=cnorm_g, cnorm_b=cnorm_b, w_out=w_out, norm2=norm2,
             w_up=w_up, ffn_conv_w=ffn_conv_w, ffn_conv_b=ffn_conv_b,
             w_down=w_down, final_norm=final_norm)
    B = x_prompt.shape[0]
    dt = x_prompt.dtype
    pool0 = jnp.zeros((DEPTH, B, POOL_MAX - 1, D_A), dt)
    conv0 = jnp.zeros((DEPTH, B, CONV_K - 1, D_C), dt)
    ffn0 = jnp.zeros((DEPTH, B, FFN_K - 1, D_FF), dt)
    y_prompt, pool_p, conv_p, ffn_p, _ = trunk(x_prompt, pool0, conv0, ffn0, 0, p)
    y_sample, pool_s, conv_s, ffn_s, v_s = trunk(x_sample, state_pool, state_conv,
                                                  state_ffn_conv, PAST_LEN, p)
    return (y_prompt, y_sample, pool_p, pool_s, conv_p, conv_s, ffn_p, ffn_s, v_s)
```

```python
import numpy as np
from contextlib import ExitStack
import concourse.bass as bass
import concourse.mybir as mybir
from concourse.bass_utils import run_bass_kernel_spmd

F32 = mybir.dt.float32
BF16 = mybir.dt.bfloat16
AF = mybir.ActivationFunctionType
ALU = mybir.AluOpType

DEPTH = 4
D = 1024
KC = 8
NP_ = 2048
NS = 64
NT = NP_ + NS
TTS = [(0, 512), (512, 512), (1024, 512), (1536, 512), (2048, 64)]
D_A, D_B, D_C = 256, 384, 384
D_FF = 2816
NFF = 22
D_IN = 1792
EPS = 1e-6
FF_GROUPS = [(0, 8), (8, 8), (16, 6)]

VT_N1 = 0
VT_N2 = VT_N1 + 32
VT_FN = VT_N2 + 32
VT_PS = VT_FN + 8
VT_CW = VT_PS + 8
VT_CB = VT_CW + 372
VT_CG = VT_CB + 12
VT_CBB = VT_CG + 12
VT_FW = VT_CBB + 12
VT_FB = VT_FW + 264
VT_N = VT_FB + 88

C_ID = 0
C_TRI = 128
C_M64 = 256
C_E4 = 320
C_INVW = 384
C_INVC = 386
C_CT = 418
C_N = 450


class Res:
    __slots__ = ("name", "w", "r", "excl")

    def __init__(self, name, inherit=None, excl=False):
        self.name = name
        self.w = None
        self.r = dict(inherit) if inherit else {}
        self.excl = excl


class DSem:
    __slots__ = ("sem", "cnt")

    def __init__(self, sem):
        self.sem = sem
        self.cnt = 0


class Sched:
    def __init__(self, nc, stack):
        self.nc = nc
        self.stack = stack
        self.E = {}
        for name, eng in [("pe", nc.tensor), ("act", nc.scalar), ("dve", nc.vector),
                          ("pool", nc.gpsimd), ("sp", nc.sync)]:
            sem = stack.enter_context(nc.semaphore("s_" + name))
            self.E[name] = {"eng": eng, "sem": sem, "cnt": 0, "waited": {}}
        self.dsems = []

    def dsem(self, name):
        d = DSem(self.stack.enter_context(self.nc.semaphore(name)))
        self.dsems.append(d)
        return d

    def _deps(self, reads, writes):
        deps = {}
        for r in reads:
            if r.w is not None and deps.get(r.w[0], 0) < r.w[1]:
                deps[r.w[0]] = r.w[1]
            if r.excl:
                for s, v in r.r.items():
                    if deps.get(s, 0) < v:
                        deps[s] = v
        for w in writes:
            if w.w is not None and deps.get(w.w[0], 0) < w.w[1]:
                deps[w.w[0]] = w.w[1]
            for s, v in w.r.items():
                if deps.get(s, 0) < v:
                    deps[s] = v
        return deps

    def _emit_waits(self, ename, deps):
        st = self.E[ename]
        for s, v in deps.items():
            if s == st["sem"] and (v > st["cnt"] or ename == "pe"):
                continue
            if st["waited"].get(s, 0) < v:
                for oe in self.E.values():
                    if oe["sem"] == s:
                        assert v <= oe["cnt"], ("wait on deferred event", ename, v, oe["cnt"])
                st["eng"].wait_ge(s, v)
                st["waited"][s] = v

    def _record(self, ev, reads, writes):
        s, v = ev
        for r in reads:
            if r.r.get(s, 0) < v:
                r.r[s] = v
        for w in writes:
            w.w = ev
            w.r = {}

    def op(self, ename, fn, reads=(), writes=(), inc=True):
        st = self.E[ename]
        self._emit_waits(ename, self._deps(reads, writes))
        ins = fn(st["eng"])
        ev = (st["sem"], st["cnt"] + 1)
        if inc:
            ins.then_inc(st["sem"], 1)
            st["cnt"] += 1
        self._record(ev, reads, writes)
        return ins

    def dma(self, qname, out, in_, ds, reads=(), writes=(), **kw):
        st = self.E[qname]
        deps = self._deps(reads, writes)
        deps.pop(ds.sem, None)
        self._emit_waits(qname, deps)
        ins = st["eng"].dma_start(out=out, in_=in_, **kw)
        ds.cnt += 16
        ins.then_inc(ds.sem, 16)
        self._record((ds.sem, ds.cnt), reads, writes)
        return ins

    def wait_all(self, ename, ress):
        deps = {}
        for r in ress:
            if r.w is not None and deps.get(r.w[0], 0) < r.w[1]:
                deps[r.w[0]] = r.w[1]
            for s, v in r.r.items():
                if deps.get(s, 0) < v:
                    deps[s] = v
        self._emit_waits(ename, deps)


class Arena:
    def __init__(self, ap, nwords):
        self.ap = ap
        self.n = nwords
        self.live = []

    def carve(self, name, start, nwords, nres=1):
        end = start + nwords
        assert 0 <= start and end <= self.n, (name, start, end, self.n)
        inh = {}
        keep = []
        for (a, b, rs) in self.live:
            if a < end and start < b:
                for r in rs:
                    if r.w is not None and inh.get(r.w[0], 0) < r.w[1]:
                        inh[r.w[0]] = r.w[1]
                    for s, v in r.r.items():
                        if inh.get(s, 0) < v:
                            inh[s] = v
                if not (start <= a and b <= end):
                    keep.append((a, b, rs))
            else:
                keep.append((a, b, rs))
        rs = [Res(f"{name}{i}", inh) for i in range(nres)]
        keep.append((start, end, rs))
        self.live = keep
        return self.ap[:, start:end], rs


class _Stop(Exception):
    pass


def build_nc(stop=None):
    nc = bass.Bass("TRN2", target_bir_lowering=False)

    def mark(k):
        if stop is not None and k == stop:
            raise _Stop()

    def din(name, shape):
        return nc.dram_tensor(name, list(shape), F32, kind="ExternalInput").ap()

    def dout(name, shape):
        return nc.dram_tensor(name, list(shape), F32, kind="ExternalOutput").ap()

    xp = din("xp", [NP_, D])
    xs = din("xs", [16, 4, D])
    sp_ = din("st_pool", [DEPTH, 16, 15, D_A])
    sc_ = din("st_conv", [DEPTH, 16, 30, D_C])
    sf_ = din("st_ffn", [DEPTH, 16, 2, D_FF])
    norm1 = din("norm1", [DEPTH, D])
    w_in = din("w_in", [DEPTH, D, D_IN])
    pool_w = din("pool_w", [DEPTH, 4, 64, 64])
    pool_scale = din("pool_scale", [DEPTH, D_A])
    sgu_w = din("sgu_w", [DEPTH, 4, 128, 128])
    sgu_b = din("sgu_b", [DEPTH, 4, 128])
    conv_w = din("conv_w", [DEPTH, 31, D_C])
    conv_b = din("conv_b", [DEPTH, D_C])
    cnorm_g = din("cnorm_g", [DEPTH, D_C])
    cnorm_b = din("cnorm_b", [DEPTH, D_C])
    w_out = din("w_out", [DEPTH, D, D])
    norm2 = din("norm2", [DEPTH, D])
    w_up = din("w_up", [DEPTH, D, 2 * D_FF])
    ffn_conv_w = din("ffn_conv_w", [DEPTH, 3, D_FF])
    ffn_conv_b = din("ffn_conv_b", [DEPTH, D_FF])
    w_down = din("w_down", [DEPTH, D_FF, D])
    final_norm = din("final_norm", [D])
    cst_d = din("cst", [128, C_N])

    y_p = dout("y_p", [NP_, D])
    y_s = dout("y_s", [16, 4, D])
    o_pool_p = dout("o_pool_p", [DEPTH, 15, D_A])
    o_pool_s = dout("o_pool_s", [DEPTH, 16, 15, D_A])
    o_conv_p = dout("o_conv_p", [DEPTH, 30, D_C])
    o_conv_s = dout("o_conv_s", [DEPTH, 16, 30, D_C])
    o_ffn_p = dout("o_ffn_p", [DEPTH, 2, D_FF])
    o_ffn_s = dout("o_ffn_s", [DEPTH, 16, 2, D_FF])
    o_v_s = dout("o_v_s", [DEPTH, 16, 4, D_B])

    with ExitStack() as st:
        S = Sched(nc, st)

        def sb(name, shape, dt):
            return st.enter_context(nc.sbuf_tensor(name, shape, dt))

        X = sb("X", [128, KC, NT], F32)
        H = sb("H", [128, KC, NT], BF16)
        NSLOT = 3
        WSA = sb("wsall", [128, NSLOT * 3072], BF16)
        WS = [WSA[:, i * 3072:(i + 1) * 3072] for i in range(NSLOT)]
        CST = sb("cstt", [128, C_N], F32)
        VT = sb("vt", [128, VT_N], F32)
        PW = sb("pw", [128, 2, 128], BF16)
        PWf = sb("pwf", [128, 2, 128], F32)
        ONES = sb("ones", [128, 128], BF16)
        IDB = sb("idb", [128, 128], BF16)
        EPST = sb("epst", [128, 1], F32)
        WT = sb("wt", [128, 4, 128], BF16)
        WTS = sb("wts", [64, 4, 64], BF16)
        W4 = sb("w4", [4, 4, 4], F32)
        BB = sb("bb", [96, 4, 192], F32)
        M1 = sb("m1", [4, 64], F32)
        STG = [sb(f"stg{i}", [128, 384], F32) for i in range(2)]
        FS = sb("fs", [128, NFF, 32], F32)
        WST = [sb(f"wst{i}", [128, 128], F32) for i in range(4)]
        AW = (nc.sbuf_bytes_remaining // 4) - 16
        ARENA_T = sb("arena", [128, AW], F32)
        AR = Arena(ARENA_T, AW)

        PS = [st.enter_context(nc.psum_tensor(f"ps{i}", [128, 512], F32)) for i in range(8)]
        RPS = [Res(f"ps{i}", excl=True) for i in range(8)]
        pstate = {"i": 0}

        reserved = set()

        def bank():
            while True:
                i = pstate["i"] % 8
                pstate["i"] += 1
                if i not in reserved:
                    pstate["last"] = i
                    return PS[i], RPS[i]

        ident = CST[:, C_ID:C_ID + 128]
        trimask = CST[:, C_TRI:C_TRI + 128]
        mask64 = CST[0:64, C_M64:C_M64 + 64]
        E4 = CST[0:4, C_E4:C_E4 + 64]

        RX = [[Res(f"x{k}_{t}") for t in range(5)] for k in range(KC)]
        RH = [Res(f"h{t}") for t in range(5)]
        RWS = [Res(f"ws{i}") for i in range(NSLOT)]
        DWS = [S.dsem(f"dws{i}") for i in range(NSLOT)]
        DWB = S.dsem("dwb")
        Rcst, Rvt, Rpw, Rpwf, Rones, Rwt, Rwts, Rbr, Rbrs, Rw4, Rm1, Rfs, Rbb = [
            Res(n) for n in "cst vt pw pwf ones wt wts br brs w4 m1 fs bb".split()]
        RSTG = [Res("stg0"), Res("stg1")]
        DSTG = [S.dsem("dstg0"), S.dsem("dstg1")]
        RWST = [Res(f"wst{i}") for i in range(4)]
        DWST = [S.dsem(f"dwst{i}") for i in range(4)]
        DM = {k: S.dsem("dm_" + k) for k in ["cst", "w4", "br", "brs", "pwf", "fst"]}
        ROUT = Res("out")
        DOUT = {k: S.dsem("do_" + k) for k in ["d2d", "cstg", "pstg", "vstg", "fo0", "fo1", "yt0", "yt1"]}

        def out_dma(dst, src, reads, key):
            S.dma("sp", dst, src, DOUT[key], reads=reads)

        wstate = {"i": 0}

        def wslot():
            i = wstate["i"] % NSLOT
            wstate["i"] += 1
            return i

        def wbig(dst, src):
            S.dma("pool", dst, src, DWB, writes=RWS)

        def wdma(slot, dst, src):
            S.dma("pool", dst, src, DWS[slot], writes=[RWS[slot]])

        try:
            S.dma("sp", CST[:], cst_d, DM["cst"], writes=[Rcst])
            S.op("dve", lambda e: e.memset(ONES[:], 1.0), writes=[Rones])
            S.op("dve", lambda e: e.memset(EPST[:], EPS), writes=[Rones])
            S.op("dve", lambda e: e.tensor_copy(IDB[:], CST[:, C_ID:C_ID + 128]), reads=[Rcst], writes=[Rones])

            vt_src = [
                (norm1.rearrange("l (k p) -> (l k) p", p=128), VT_N1),
                (norm2.rearrange("l (k p) -> (l k) p", p=128), VT_N2),
                (final_norm.rearrange("(k p) -> k p", p=128), VT_FN),
                (pool_scale.rearrange("l (k p) -> (l k) p", p=128), VT_PS),
                (conv_w.rearrange("l i (k p) -> (l i k) p", p=128), VT_CW),
                (conv_b.rearrange("l (k p) -> (l k) p", p=128), VT_CB),
                (cnorm_g.rearrange("l (k p) -> (l k) p", p=128), VT_CG),
                (cnorm_b.rearrange("l (k p) -> (l k) p", p=128), VT_CBB),
                (ffn_conv_w.rearrange("l i (k p) -> (l i k) p", p=128), VT_FW),
                (ffn_conv_b.rearrange("l (k p) -> (l k) p", p=128), VT_FB),
            ]
            bi = 0
            for src, off in vt_src:
                R = src.shape[0]
                for r0 in range(0, R, 128):
                    r = min(128, R - r0)
                    s_ = bi % 4
                    bi += 1
                    S.dma("sp", WST[s_][0:r, :], src[r0:r0 + r, :], DWST[s_], writes=[RWST[s_]])
                    pb, Rpb = bank()
                    S.op("pe", lambda e: e.transpose(pb[:, 0:r], WST[s_][0:r, :], ident[0:r, 0:r]),
                         reads=[RWST[s_], Rcst], writes=[Rpb])
                    S.op("act", lambda e: e.copy(VT[:, off + r0:off + r0 + r], pb[:, 0:r]), reads=[Rpb], writes=[Rvt])

            mark('xload')
            def norm_begin():
                sq_v, Rsq = AR.carve("sq", 0, 3 * 256, 3)
                rt_v, Rrt = AR.carve("rt", 768, 2 * 512, 2)
                ri_v, Rri = AR.carve("ri", 768 + 1024, 2 * 512, 2)
                return (sq_v.bitcast(BF16).rearrange("p (a n) -> p a n", a=3), Rsq, rt_v, Rrt, ri_v, Rri)

            def norm_tt(ctx, gcol, ti, final=False, ycb=None, defer=False):
                sqb, Rsq, rt_v, Rrt, ri_v, Rri = ctx
                c0, n = TTS[ti]
                pb, Rpb = bank()
                bidx = pstate["last"]
                if defer:
                    reserved.add(bidx)
                for kc in range(KC):
                    q = kc % 3
                    S.op("act", lambda e: e.activation(sqb[:, q, 0:n], X[:, kc, c0:c0 + n], AF.Square),
                         reads=[RX[kc][ti]], writes=[Rsq[q]])
                    S.op("pe", lambda e: e.matmul(pb[:, 0:n], ONES[:], sqb[:, q, 0:n], start=(kc == 0), stop=(kc == KC - 1)),
                         reads=[Rsq[q], Rones], writes=[Rpb])
                b = ti % 2
                rt = rt_v[:, b * 512:b * 512 + n]
                ri = ri_v[:, b * 512:b * 512 + n]

                def tail():
                    S.op("act", lambda e: e.activation(rt, pb[:, 0:n], AF.Ln, bias=EPST[:, 0:1], scale=1.0 / D),
                         reads=[Rpb, Rones], writes=[Rrt[b]])
                    S.op("act", lambda e: e.activation(ri, rt, AF.Exp, scale=-0.5), reads=[Rrt[b]], writes=[Rri[b]])
                    if not final:
                        for kc in range(KC):
                            S.op("dve", lambda e: e.scalar_tensor_tensor(H[:, kc, c0:c0 + n], X[:, kc, c0:c0 + n],
                                                                         VT[:, gcol + kc:gcol + kc + 1], ri, ALU.mult, ALU.mult),
                                 reads=[RX[kc][ti], Rri[b], Rvt], writes=[RH[ti]])
                    else:
                        ycb(ti, c0, n, ri, Rri[b])
                    reserved.discard(bidx)
                if defer:
                    return tail
                tail()

            def rmsnorm(gcol, final=False, ycb=None):
                ctx = norm_begin()
                for ti in range(5):
                    norm_tt(ctx, gcol, ti, final, ycb)

            def prep_dma(l):
                for h in range(4):
                    S.dma("sp", WST[h][:], sgu_w[l, h], DWST[h], writes=[RWST[h]])
                S.dma("sp", W4[:], sgu_w[l, :, 0:4, 0:4].rearrange("h a b -> a h b"), DM["w4"], writes=[Rw4])
                S.dma("sp", BB[:, :, 0:128], sgu_b[l].partition_broadcast(96), DM["br"], writes=[Rbb])

            def prep(l):
                for h in range(4):
                    s_ = h
                    pb, Rpb = bank()
                    S.op("pe", lambda e: e.transpose(pb[:, 0:128], WST[s_][:], ident), reads=[RWST[s_], Rcst], writes=[Rpb])
                    S.op("dve", lambda e: e.tensor_tensor(WT[:, h, :], pb[:, 0:128], trimask, ALU.mult),
                         reads=[Rpb, Rcst], writes=[Rwt])
                for h in range(4):
                    S.op("dve", lambda e: e.tensor_copy(BB[:, h, 128:192].rearrange("p (t b) -> p t b", b=16),
                                                        BB[:, h, 0:4].unsqueeze(2).broadcast_to([96, 4, 16])),
                         reads=[Rbb], writes=[Rbb])
                for h in range(4):
                    pb, Rpb = bank()
                    S.op("pe", lambda e: e.matmul(pb[0:4, 0:64], W4[:, h, :], E4, start=True, stop=True),
                         reads=[Rw4, Rcst], writes=[Rpb])
                    S.op("act", lambda e: e.copy(M1[:], pb[0:4, 0:64]), reads=[Rpb], writes=[Rm1])
                    pb2, Rpb2 = bank()
                    S.op("pe", lambda e: e.matmul(pb2[0:64, 0:64], E4, M1[:], start=True, stop=True),
                         reads=[Rm1, Rcst], writes=[Rpb2])
                    S.op("dve", lambda e: e.tensor_tensor(WTS[:, h, :], pb2[0:64, 0:64], mask64, ALU.mult),
                         reads=[Rpb2, Rcst], writes=[Rwts])
                S.op("dve", lambda e: e.memset(PWf[:], 0.0), writes=[Rpwf])
                for g in range(4):
                    j, q = g // 2, g % 2
                    S.dma("sp", PWf[64 * q:64 * q + 64, j, 64 * q:64 * q + 64], pool_w[l, g], DM["pwf"], writes=[Rpwf])
                S.op("dve", lambda e: e.tensor_copy(PW[:], PWf[:]), reads=[Rpwf], writes=[Rpw])


            def make_final_cb():
                yf_v, Ryf = AR.carve("yf", 2816, 4096, 1)
                yt0_v, Ryt0 = AR.carve("yt0", 2816 + 4096, 1024, 1)
                yt1_v, Ryt1 = AR.carve("yt1", 8900 + 6 * NT // 2, 1024, 1)
                yts = [yt0_v, yt1_v]
                Ryt = [Ryt0[0], Ryt1[0]]
                ycnt = {"i": 0}

                def final_cb(ti, c0, n, ri, Rri_):
                    yb = 0
                    Y = yf_v[:, 0:4096].rearrange("p (k n) -> p k n", k=8)
                    for kc in range(KC):
                        S.op("dve", lambda e: e.scalar_tensor_tensor(Y[:, kc, 0:n], X[:, kc, c0:c0 + n], VT[:, VT_FN + kc:VT_FN + kc + 1], ri,
                                                                     ALU.mult, ALU.mult),
                             reads=[RX[kc][ti], Rri_, Rvt], writes=[Ryf[yb]])
                    def T_stage(ti=ti, c0=c0, n=n):
                        nsub = 4 if ti < 4 else 1
                        rows = 128 if ti < 4 else 64
                        for a in range(nsub):
                            q = ycnt["i"] % 2
                            ycnt["i"] += 1
                            YT = yts[q]
                            for half in range(2):
                                pb, Rpb = bank()
                                for kq in range(4):
                                    kc = half * 4 + kq
                                    S.op("pe", lambda e: e.transpose(pb[0:rows, kq * 128:(kq + 1) * 128], Y[:, kc, a * 128:a * 128 + rows], ident),
                                         reads=[Ryf[yb], Rcst], writes=[Rpb], inc=(kq == 3))
                                if half == 0:
                                    S.op("act", lambda e: e.copy(YT[0:rows, 0:512], pb[0:rows, :]), reads=[Rpb], writes=[Ryt[q]])
                                else:
                                    S.op("dve", lambda e: e.tensor_copy(YT[0:rows, 512:1024], pb[0:rows, :]), reads=[Rpb], writes=[Ryt[q]])
                            if ti < 4:
                                out_dma(y_p[c0 + a * 128:c0 + (a + 1) * 128, :], YT[:, :], [Ryt[q]], f"yt{q}")
                            else:
                                for t in range(4):
                                    out_dma(y_s[:, t, :], YT[16 * t:16 * t + 16, :], [Ryt[q]], f"yt{q}")
                    Tq[ti] = T_stage
                Tq = {}
                return final_cb, Tq

            xs_v, Rxs = AR.carve("xstage", 2816, 2 * 4096, 2)
            xst = [xs_v[:, 0:4096].rearrange("p (a d) -> p a d", a=4), xs_v[:, 4096:8192].rearrange("p (a d) -> p a d", a=4)]
            dxs = [S.dsem("dxs0"), S.dsem("dxs1")]
            nctx0 = norm_begin()
            for ti, (c0, n) in enumerate(TTS):
                s_ = ti % 2
                if ti < 4:
                    S.dma("sp", xst[s_], xp[c0:c0 + 512, :].rearrange("(a p) d -> p a d", p=128), dxs[s_], writes=[Rxs[s_]])
                    na, rows = 4, 128
                else:
                    for t in range(4):
                        S.dma("sp", xst[s_][16 * t:16 * t + 16, 0, :], xs[:, t, :], dxs[s_], writes=[Rxs[s_]])
                    na, rows = 1, 64
                for kc in range(KC):
                    pb, Rpb = bank()
                    for a in range(na):
                        S.op("pe", lambda e: e.transpose(pb[:, a * 128:a * 128 + rows], xst[s_][0:rows, a, kc * 128:(kc + 1) * 128],
                                                         ident[0:rows, 0:rows]),
                             reads=[Rxs[s_], Rcst], writes=[Rpb], inc=(a == na - 1))
                    eng = "act" if kc % 2 == 0 else "dve"
                    if eng == "act":
                        S.op("act", lambda e: e.copy(X[:, kc, c0:c0 + n], pb[:, 0:n]), reads=[Rpb], writes=[RX[kc][ti]])
                    else:
                        S.op("dve", lambda e: e.tensor_copy(X[:, kc, c0:c0 + n], pb[:, 0:n]), reads=[Rpb], writes=[RX[kc][ti]])
                if ti >= 1:
                    norm_tt(nctx0, VT_N1, ti - 1)
            norm_tt(nctx0, VT_N1, 4)

            for l in range(DEPTH):
                Win = w_in[l].rearrange("(k p) n -> p k n", p=128)
                Wout = w_out[l]
                Wup = w_up[l].rearrange("(k p) n -> p k n", p=128)
                Wdn = w_down[l].rearrange("(k p) n -> p k n", p=128)

                if l == 0:
                    prep_dma(0)
                    prep(0)
                mark(f'prep{l}')

                mark(f'n1_{l}')
                CPW = 30 + NP_
                CSW = 34 * 16
                CW_ = CPW + CSW
                BB_ = AW - 9504
                AB_ = AW - 5280
                CB_ = AW - 3168
                cin_v, Rcin = AR.carve("cin", 0, 3 * CW_ // 2, 3 * 6)
                cin = cin_v.bitcast(BF16).rearrange("p (j n) -> p j n", j=3)
                d0_ = 3 * CW_ // 2
                dg_v, Rdg = AR.carve("dg", d0_, 3 * 31 * 64, 3)
                dgb = dg_v.bitcast(BF16).rearrange("p (b i n) -> p b i n", b=3, i=31)
                a1 = d0_ + 3 * 31 * 64
                sig_v, Rsig = AR.carve("sig", a1, 2 * 512, 2)
                a2 = a1 + 1024
                cst_v, Rcstg = AR.carve("cstg", a2, 384, 1)

                def RC(j, k):
                    return Rcin[j * 6 + k]

                for j in range(3):
                    S.op("dve", lambda e: e.memset(cin[:, j, 0:30], 0.0), writes=[RC(j, 0)])
                def cst_dma(q):
                    S.dma("sp", STG[q % 2][0:120, 0:384], sc_[l, 4 * q:4 * q + 4].rearrange("b i c -> (b i) c"), DSTG[q % 2], writes=[RSTG[q % 2]])

                def cst_tr(q):
                    s_ = q % 2
                    for j2 in range(3):
                        pb, Rpb = bank()
                        S.op("pe", lambda e: e.transpose(pb[:, 0:120], STG[s_][0:120, j2 * 128:(j2 + 1) * 128], ident[0:120, 0:120]),
                             reads=[RSTG[s_], Rcst], writes=[Rpb])
                        dst = cin[:, j2, CPW:CPW + 480].rearrange("p (i b) -> p i b", b=16)[:, :, 4 * q:4 * q + 4]
                        srcv = pb[:, 0:120].rearrange("p (b i) -> p i b", i=30)
                        S.op("act", lambda e: e.copy(dst, srcv), reads=[Rpb], writes=[RC(j2, 5)])
                cst_dma(0)
                cst_dma(1)
                cslabs = []
                for j in range(3):
                    sl = wslot()
                    wv = WS[sl][:, 0:2048].rearrange("p (k n) -> p k n", k=8)
                    wdma(sl, wv[:, :, 0:128], Win[:, :, 1024 + 128 * j:1024 + 128 * j + 128])
                    wdma(sl, wv[:, :, 128:256], Win[:, :, 1408 + 128 * j:1408 + 128 * j + 128])
                    cslabs.append((sl, wv))
                for j in range(3):
                    for i in range(31):
                        wcol = VT[:, VT_CW + (l * 31 + i) * 3 + j:VT_CW + (l * 31 + i) * 3 + j + 1]
                        S.op("pool", lambda e: e.tensor_scalar(dgb[:, j, i, :], IDB[:], wcol, 0.0, ALU.mult, ALU.add),
                             reads=[Rvt, Rones], writes=[Rdg[j]])
                out_dma(o_conv_s[l][:, 0:26, :], sc_[l][:, 4:30, :], [], "d2d")
                for j in range(3):
                    sl, wv = cslabs[j]
                    for ti, (c0, n) in enumerate(TTS):
                        pg, Rpg = bank()
                        for k in range(KC):
                            S.op("pe", lambda e: e.matmul(pg[:, 0:n], wv[:, k, 128:256], H[:, k, c0:c0 + n], start=(k == 0), stop=(k == 7)),
                                 reads=[RWS[sl], RH[ti]], writes=[Rpg], inc=(k == 7))
                        pv, Rpv = bank()
                        for k in range(KC):
                            S.op("pe", lambda e: e.matmul(pv[:, 0:n], wv[:, k, 0:128], H[:, k, c0:c0 + n], start=(k == 0), stop=(k == 7)),
                                 reads=[RWS[sl], RH[ti]], writes=[Rpv], inc=(k == 7))
                        b = ti % 2
                        sg = sig_v[:, b * 512:b * 512 + n]
                        S.op("act", lambda e: e.activation(sg, pg[:, 0:n], AF.Sigmoid), reads=[Rpg], writes=[Rsig[b]])
                        if ti < 4:
                            dst = cin[:, j, 30 + c0:30 + c0 + n]
                        else:
                            dst = cin[:, j, CPW + 480:CPW + 544]
                        S.op("dve", lambda e: e.tensor_tensor(dst, pv[:, 0:n], sg, ALU.mult), reads=[Rpv, Rsig[b]], writes=[RC(j, 1 + ti)])
                    if j == 0:
                        cst_tr(0)
                        cst_tr(1)
                        cst_dma(2)
                        cst_dma(3)
                    elif j == 1:
                        cst_tr(2)
                        cst_tr(3)
                    pt, Rpt = bank()
                    for k in range(KC):
                        S.op("pe", lambda e: e.matmul(pt[0:94, 0:256], H[:, k, NT - 94:NT], wv[:, k, 0:256], start=(k == 0), stop=(k == 7)),
                             reads=[RWS[sl], RH[3], RH[4]], writes=[Rpt], inc=(k == 7))
                    sg = sig_v[0:94, 0:128]
                    S.op("act", lambda e: e.activation(sg, pt[0:94, 128:256], AF.Sigmoid), reads=[Rpt], writes=[Rsig[0]])
                    S.op("dve", lambda e: e.tensor_tensor(cst_v[0:94, j * 128:(j + 1) * 128], pt[0:94, 0:128], sg, ALU.mult),
                         reads=[Rpt, Rsig[0]], writes=[Rcstg[0]])
                out_dma(o_conv_p[l], cst_v[0:30, 0:384], [Rcstg[0]], "cstg")
                for t in range(4):
                    out_dma(o_conv_s[l][:, 26 + t, :], cst_v[30 + 16 * t:46 + 16 * t, 0:384], [Rcstg[0]], "cstg")

                mark(f'cin{l}')
                mark(f'conv{l}')
                ln0_ = a1
                ybf_v, Rybf = AR.carve("ybf", ln0_, 4 * 256, 4)
                ybf = ybf_v.bitcast(BF16).rearrange("p (a n) -> p a n", a=4)
                lt_v, Rlt = AR.carve("lnt", ln0_ + 1024, 4 * 512, 4)
                yt_v2, Ryt2 = AR.carve("lnyt", ln0_ + 1024 + 2048, 2 * 512, 2)
                assert ln0_ + 1024 + 2048 + 1024 <= CB_, (ln0_, CB_)
                cout_v, Rco = AR.carve("cout", CB_, 3 * NT // 2, 5)
                cout = cout_v.bitcast(BF16).rearrange("p (j n) -> p j n", j=3)
                for pos, ti in enumerate([4, 0, 1, 2, 3]):
                    c0, n = TTS[ti]
                    pjs = []
                    for j in range(3):
                        pb, Rpb = bank()
                        pjs.append((pb, Rpb))
                        for i in range(31):
                            if ti < 4:
                                src = cin[:, j, i + c0:i + c0 + n]
                                rr = [RC(j, k) for k in range(5)]
                            else:
                                src = cin[:, j, CPW + 16 * i:CPW + 16 * i + 64]
                                rr = [RC(j, 5)]
                            S.op("pe", lambda e: e.matmul(pb[:, 0:n], dgb[:, j, i, :], src, start=(i == 0), stop=(i == 30)),
                                 reads=rr + [Rdg[j]], writes=[Rpb], inc=(i == 30))
                    pm, Rpm = bank()
                    pq, Rpq = bank()
                    for j in range(3):
                        pb, Rpb = pjs[j]
                        cbias = VT[:, VT_CB + l * 3 + j:VT_CB + l * 3 + j + 1]
                        q0, q1 = (2 * j) % 4, (2 * j + 1) % 4
                        S.op("act", lambda e: e.activation(ybf[:, q0, 0:n], pb[:, 0:n], AF.Identity, bias=cbias), reads=[Rpb, Rvt], writes=[Rybf[q0]])
                        S.op("pe", lambda e: e.matmul(pm[:, 0:n], ONES[:], ybf[:, q0, 0:n], start=(j == 0), stop=(j == 2)),
                             reads=[Rybf[q0], Rones], writes=[Rpm])
                        S.op("act", lambda e: e.activation(ybf[:, q1, 0:n], pb[:, 0:n], AF.Square, bias=cbias), reads=[Rpb, Rvt], writes=[Rybf[q1]])
                        S.op("pe", lambda e: e.matmul(pq[:, 0:n], ONES[:], ybf[:, q1, 0:n], start=(j == 0), stop=(j == 2)),
                             reads=[Rybf[q1], Rones], writes=[Rpq])
                    b = (pos % 2) * 2
                    mu = lt_v[:, b * 512:b * 512 + n]
                    rs = lt_v[:, (b + 1) * 512:(b + 1) * 512 + n]
                    S.op("act", lambda e: e.mul(mu, pm[:, 0:n], 1.0 / D_C), reads=[Rpm], writes=[Rlt[b]])
                    S.op("dve", lambda e: e.tensor_tensor(rs, mu, mu, ALU.mult), reads=[Rlt[b]], writes=[Rlt[b + 1]])
                    S.op("dve", lambda e: e.scalar_tensor_tensor(rs, pq[:, 0:n], 1.0 / D_C, rs, ALU.mult, ALU.subtract),
                         reads=[Rpq, Rlt[b + 1]], writes=[Rlt[b + 1]])
                    S.op("dve", lambda e: e.tensor_scalar(rs, rs, 0.0, None, ALU.max), reads=[Rlt[b + 1]], writes=[Rlt[b + 1]])
                    S.op("act", lambda e: e.activation(rs, rs, AF.Ln, bias=EPST[:, 0:1], scale=1.0), reads=[Rlt[b + 1], Rones], writes=[Rlt[b + 1]])
                    S.op("act", lambda e: e.activation(rs, rs, AF.Exp, scale=-0.5), reads=[Rlt[b + 1]], writes=[Rlt[b + 1]])
                    for j in range(3):
                        pb, Rpb = pjs[j]
                        cbias = VT[:, VT_CB + l * 3 + j:VT_CB + l * 3 + j + 1]
                        yq = (ti * 3 + j) % 2
                        a_ = yt_v2[:, yq * 512:yq * 512 + n]
                        S.op("dve", lambda e: e.scalar_tensor_tensor(a_, pb[:, 0:n], cbias, mu, ALU.add, ALU.subtract),
                             reads=[Rpb, Rlt[b], Rvt], writes=[Ryt2[yq]])
                        S.op("dve", lambda e: e.tensor_tensor(a_, a_, rs, ALU.mult), reads=[Rlt[b + 1]], writes=[Ryt2[yq]])
                        gcol = VT[:, VT_CG + l * 3 + j:VT_CG + l * 3 + j + 1]
                        bcol = VT[:, VT_CBB + l * 3 + j:VT_CBB + l * 3 + j + 1]
                        S.op("act", lambda e: e.activation(cout[:, j, c0:c0 + n], a_, AF.Silu, bias=bcol, scale=gcol),
                             reads=[Ryt2[yq], Rvt], writes=[Rco[ti]])

                mark(f'ln{l}')
                PPW = 15 + NP_
                PSW = 19 * 16
                ZW = PPW + PSW
                aout_v, Rao = AR.carve("aout", AB_, NT, 5)
                aout = aout_v.bitcast(BF16).rearrange("p (j n) -> p j n", j=2)
                b1 = 0
                zab_v, Rzab = AR.carve("zab", b1, ZW, 14)
                zab = zab_v.bitcast(BF16).rearrange("p (j n) -> p j n", j=2)
                wp_v, Rwp = AR.carve("wp", b1 + ZW, 2 * 16 * 64, 2)
                wpb = wp_v.bitcast(BF16).rearrange("p (j i n) -> p j i n", j=2, i=16)
                sm_v, Rsm = AR.carve("poolsm", b1 + ZW + 2048, 136, 4)
                pstg_v, Rpstg = AR.carve("pstg", b1 + ZW + 2048 + 136, 256, 1)
                fst_v, Rfst = AR.carve("fst", b1 + ZW + 2048 + 136 + 256, 2816, 1)
                sl = wslot()
                wv = WS[sl][:, 0:2048].rearrange("p (k n) -> p k n", k=8)
                wdma(sl, wv, Win[:, :, 0:256])
                for q in range(2):
                    S.dma("sp", STG[q][0:120, 0:256], sp_[l, 8 * q:8 * q + 8].rearrange("b i c -> (b i) c"), DSTG[q], writes=[RSTG[q]])
                S.dma("sp", fst_v[0:32, 0:2816], sf_[l].rearrange("b i c -> (b i) c"), DM["fst"], writes=[Rfst[0]])
                out_dma(o_pool_s[l][:, 0:11, :], sp_[l][:, 4:15, :], [], "d2d")
                WMAX = [4, 16]
                for j in range(2):
                    for i in range(WMAX[j]):
                        S.op("pool", lambda e: e.tensor_scalar(wpb[:, j, i, :], PWf[:, j, :], CST[:, C_CT + 16 * j + i:C_CT + 16 * j + i + 1], 0.0, ALU.mult, ALU.add),
                             reads=[Rpwf, Rcst], writes=[Rwp[j]])
                zsA = [sm_v[:, 0:31], sm_v[:, 31:62]]
                RzsA = [Rsm[0], Rsm[1]]
                for j in range(2):
                    RZ = Rzab[7 * j:7 * j + 7]
                    zs = zsA[j]
                    S.op("dve", lambda e: e.memset(zab[:, j, 0:15], 0.0), writes=[RZ[0]])
                    S.op("dve", lambda e: e.memset(zs[:, 0:15], 0.0), writes=[RzsA[j]])
                    for ti, (c0, n) in enumerate(TTS):
                        pb, Rpb = bank()
                        for k in range(KC):
                            S.op("pe", lambda e: e.matmul(pb[:, 0:n], wv[:, k, j * 128:(j + 1) * 128], H[:, k, c0:c0 + n], start=(k == 0), stop=(k == 7)),
                                 reads=[RWS[sl], RH[ti]], writes=[Rpb], inc=(k == 7))
                        dst = zab[:, j, 15 + c0:15 + c0 + n] if ti < 4 else zab[:, j, PPW + 240:PPW + 304]
                        S.op("act", lambda e: e.copy(dst, pb[:, 0:n]), reads=[Rpb], writes=[RZ[1 + ti]])
                        if ti == 0:
                            S.op("dve", lambda e: e.tensor_copy(zs[:, 15:31], pb[:, 0:16]), reads=[Rpb], writes=[RzsA[j]])
                for j in range(2):
                    RZ = Rzab[7 * j:7 * j + 7]
                    zs = zsA[j]
                    ssa = sm_v[:, 62:93]
                    ssb = sm_v[:, 93:124]
                    pf = sm_v[:, 124:132].bitcast(BF16)
                    for q in range(2):
                        s_ = q % 2
                        pb, Rpb = bank()
                        S.op("pe", lambda e: e.transpose(pb[:, 0:120], STG[s_][0:120, j * 128:(j + 1) * 128], ident[0:120, 0:120]),
                             reads=[RSTG[s_], Rcst], writes=[Rpb])
                        dst = zab[:, j, PPW:PPW + 240].rearrange("p (i b) -> p i b", b=16)[:, :, 8 * q:8 * q + 8]
                        S.op("act", lambda e: e.copy(dst, pb[:, 0:120].rearrange("p (b i) -> p i b", i=15)), reads=[Rpb], writes=[RZ[6]])
                    for ti, (c0, n) in enumerate(TTS):
                        pb, Rpb = bank()
                        for i in range(WMAX[j]):
                            if ti < 4:
                                src = zab[:, j, 15 + c0 - i:15 + c0 - i + n]
                            else:
                                src = zab[:, j, PPW + 240 - 16 * i:PPW + 304 - 16 * i]
                            S.op("pe", lambda e: e.matmul(pb[:, 0:n], wpb[:, j, i, :], src, start=(i == 0), stop=(i == WMAX[j] - 1)),
                                 reads=([RZ[ti], RZ[1 + ti]] if ti < 4 else [RZ[5], RZ[6]]) + [Rwp[j]], writes=[Rpb], inc=(i == WMAX[j] - 1))
                        S.op("act", lambda e: e.activation(aout[:, j, c0:c0 + n], pb[:, 0:n], AF.Identity,
                                                           scale=VT[:, VT_PS + l * 2 + j:VT_PS + l * 2 + j + 1]),
                             reads=[Rpb, Rvt], writes=[Rao[ti]])
                    def wsum_s(dst_v, src_v, sh, p0, p1, rd, wr):
                        S.op("dve", lambda e: e.tensor_tensor(dst_v[p0:p1, sh:31], src_v[p0:p1, sh:31], src_v[p0:p1, 0:31 - sh], ALU.add),
                             reads=rd, writes=wr)
                    R0, R1, R2, R3 = [RzsA[j]], [Rsm[2]], [Rsm[3]], [Rsm[3]]
                    if j == 0:
                        wsum_s(ssa, zs, 1, 64, 128, R0, R1)
                        wsum_s(ssb, zs, 1, 0, 64, R0, R2)
                        wsum_s(ssb, ssa, 2, 64, 128, R1, R2)
                    else:
                        wsum_s(ssa, zs, 1, 0, 128, R0, R1)
                        wsum_s(ssb, ssa, 2, 0, 128, R1, R2)
                        wsum_s(ssa, ssb, 4, 0, 128, R2, R1)
                        wsum_s(ssb, ssa, 8, 64, 128, R1, R2)
                        S.op("dve", lambda e: e.tensor_copy(ssb[0:64, 15:31], ssa[0:64, 15:31]), reads=R1, writes=R2)
                    ic = CST[:, C_INVC + 16 * j:C_INVC + 16 * j + 16]
                    S.op("dve", lambda e: e.tensor_tensor(ssb[:, 15:31], ssb[:, 15:31], ic, ALU.mult), reads=[Rcst], writes=R2)
                    S.op("dve", lambda e: e.tensor_tensor(pf[:, 0:16], ssb[:, 15:31], zs[:, 15:31], ALU.subtract), reads=R2 + R0, writes=R3)
                    pb, Rpb = bank()
                    S.op("pe", lambda e: e.matmul(pb[:, 0:16], PW[:, j, :], pf[:, 0:16], start=True, stop=True), reads=R3 + [Rpw], writes=[Rpb])
                    S.op("act", lambda e: e.activation(aout[:, j, 0:16], pb[:, 0:16], AF.Identity,
                                                       scale=VT[:, VT_PS + l * 2 + j:VT_PS + l * 2 + j + 1]),
                         reads=[Rpb, Rvt], writes=[Rao[0]])
                pt, Rpt = bank()
                for k in range(KC):
                    S.op("pe", lambda e: e.matmul(pt[0:79, 0:256], H[:, k, NT - 79:NT], wv[:, k, 0:256], start=(k == 0), stop=(k == 7)),
                         reads=[RWS[sl], RH[3], RH[4]], writes=[Rpt], inc=(k == 7))
                S.op("act", lambda e: e.copy(pstg_v[0:79, 0:256], pt[0:79, 0:256]), reads=[Rpt], writes=[Rpstg[0]])
                out_dma(o_pool_p[l], pstg_v[0:15, 0:256], [Rpstg[0]], "pstg")
                for t in range(4):
                    out_dma(o_pool_s[l][:, 11 + t, :], pstg_v[15 + 16 * t:31 + 16 * t, 0:256], [Rpstg[0]], "pstg")

                for f in range(NFF):
                    pb, Rpb = bank()
                    S.op("pe", lambda e: e.transpose(pb[:, 0:32], fst_v[0:32, f * 128:(f + 1) * 128], ident[0:32, 0:32]),
                         reads=[Rfst[0], Rcst], writes=[Rpb])
                    S.op("act", lambda e: e.copy(FS[:, f, :].rearrange("p (i b) -> p i b", b=16), pb[:, 0:32].rearrange("p (b i) -> p i b", i=2)),
                         reads=[Rpb], writes=[Rfs])
                mark(f'ga{l}')
                bout_v, Rbo = AR.carve("bout", BB_, 2 * NT, 5)
                bout = bout_v.bitcast(BF16).rearrange("p (h n) -> p h n", h=4)
                c1_ = 0
                vb_v, Rvb = AR.carve("vb", c1_, 17 * 192, 17)
                vb = vb_v.bitcast(BF16).rearrange("p (c n) -> p c n", c=17)
                c2_ = c1_ + 17 * 192
                tt_v, Rtt = AR.carve("ttmp", c2_, 2 * 512, 2)
                vs_v, Rvs = AR.carve("vstg", c2_ + 1024, 384, 1)
                slv = wslot()
                wvv = WS[slv][:, 0:3072].rearrange("p (k n) -> p k n", k=8)
                wdma(slv, wvv, Win[:, :, 640:1024])
                slu = wslot()
                wvu = WS[slu][:, 0:3072].rearrange("p (k n) -> p k n", k=8)
                wdma(slu, wvu, Win[:, :, 256:640])
                mark(f'gbva{l}')
                for c in range(17):
                    if c == 1:
                        mark(f'gbvb{l}')
                    if c == 16:
                        mark(f'gbvc{l}')
                    rows = 128 if c < 16 else 64
                    ti = c // 4
                    pb, Rpb = bank()
                    for k in range(KC):
                        S.op("pe", lambda e: e.matmul(pb[0:rows, 0:384], H[:, k, c * 128:c * 128 + rows], wvv[:, k, :], start=(k == 0), stop=(k == 7)),
                             reads=[RWS[slv], RH[ti]], writes=[Rpb], inc=(k == 7))
                    S.op("act", lambda e: e.copy(vb[0:rows, c, :], pb[0:rows, 0:384]), reads=[Rpb], writes=[Rvb[c]])
                    if c == 16:
                        S.op("dve", lambda e: e.tensor_copy(vs_v[0:64, 0:384], pb[0:64, 0:384]), reads=[Rpb], writes=[Rvs[0]])
                        for t in range(4):
                            out_dma(o_v_s[l][:, t, :], vs_v[16 * t:16 * t + 16, 0:384], [Rvs[0]], "vstg")
                mark(f'gbv{l}')
                for h in range(4):
                    for ti, (c0, n) in enumerate(TTS):
                        if ti == 4:
                            mark(f'gbh{l}')
                        pS, RpS = bank()
                        b = ti % 2
                        T_ = tt_v[0:96, b * 512:b * 512 + n]
                        if ti < 4:
                            for cc in range(4):
                                c = ti * 4 + cc
                                S.op("pe", lambda e: e.matmul(pS[0:96, cc * 128:(cc + 1) * 128], vb[:, c, h * 96:(h + 1) * 96], WT[:, h, :],
                                                              start=True, stop=True, skip_group_check=True),
                                     reads=[Rvb[c], Rwt], writes=[RpS], inc=(cc == 3))
                            S.op("dve", lambda e: e.tensor_tensor(T_.rearrange("p (c n) -> p c n", c=4), pS[0:96, 0:512].rearrange("p (c n) -> p c n", c=4),
                                                                  BB[:, h, 0:128].unsqueeze(1).broadcast_to([96, 4, 128]), ALU.add),
                                 reads=[RpS, Rbb], writes=[Rtt[b]])
                        else:
                            S.op("pe", lambda e: e.matmul(pS[0:96, 0:64], vb[0:64, 16, h * 96:(h + 1) * 96], WTS[:, h, :],
                                                          start=True, stop=True),
                                 reads=[Rvb[16], Rwts], writes=[RpS])
                            S.op("dve", lambda e: e.tensor_tensor(T_, pS[0:96, 0:64], BB[:, h, 128:192], ALU.add),
                                 reads=[RpS, Rbb], writes=[Rtt[b]])
                        pU, RpU = bank()
                        for k in range(KC):
                            S.op("pe", lambda e: e.matmul(pU[0:96, 0:n], wvu[:, k, h * 96:(h + 1) * 96], H[:, k, c0:c0 + n], start=(k == 0), stop=(k == 7)),
                                 reads=[RWS[slu], RH[ti]], writes=[RpU], inc=(k == 7))
                        S.op("dve", lambda e: e.tensor_tensor(bout[0:96, h, c0:c0 + n], pU[0:96, 0:n], T_, ALU.mult),
                             reads=[RpU, Rtt[b]], writes=[Rbo[ti]])

                mark(f'gb{l}')
                for mm in range(4):
                    sl = wslot()
                    wo = WS[sl][:, 0:2304].rearrange("p (k n) -> p k n", k=9)
                    cs = slice(mm * 256, mm * 256 + 256)
                    wdma(sl, wo[:, 0:2, :], Wout[0:256, cs].rearrange("(k p) n -> p k n", p=128))
                    wdma(sl, wo[0:96, 2:6, :], Wout[256:640, cs].rearrange("(k p) n -> p k n", p=96))
                    wdma(sl, wo[:, 6:9, :], Wout[640:1024, cs].rearrange("(k p) n -> p k n", p=128))
                    def oproj_block(m2, ti):
                        m = mm * 2 + m2
                        ms = slice(m2 * 128, m2 * 128 + 128)
                        c0, n = TTS[ti]
                        pb, Rpb = bank()
                        ops = []
                        for j in range(2):
                            ops.append((wo[:, j, ms], aout[:, j, c0:c0 + n], Rao[ti]))
                        for h in range(4):
                            ops.append((wo[0:96, 2 + h, ms], bout[0:96, h, c0:c0 + n], Rbo[ti]))
                        for j in range(3):
                            ops.append((wo[:, 6 + j, ms], cout[:, j, c0:c0 + n], Rco[ti]))
                        for q, (lh, rh, rr) in enumerate(ops):
                            S.op("pe", lambda e: e.matmul(pb[:, 0:n], lh, rh, start=(q == 0), stop=(q == 8)),
                                 reads=[RWS[sl], rr], writes=[Rpb], inc=(q == 8))
                        S.op("dve", lambda e: e.tensor_tensor(X[:, m, c0:c0 + n], pb[:, 0:n], X[:, m, c0:c0 + n], ALU.add),
                             reads=[Rpb], writes=[RX[m][ti]])

                    if mm < 3:
                        for m2 in range(2):
                            for ti in range(5):
                                oproj_block(m2, ti)
                    else:
                        nctx2 = norm_begin()
                        o_order = [4, 0, 1, 2, 3]
                        tl2 = {}
                        for pos, ti in enumerate(o_order):
                            for m2 in range(2):
                                oproj_block(m2, ti)
                            if pos >= 1:
                                tl2[pos - 1] = norm_tt(nctx2, VT_N2 + l * 8, o_order[pos - 1], defer=True)
                            if pos >= 2:
                                tl2[pos - 2]()
                        tl2[4] = norm_tt(nctx2, VT_N2 + l * 8, o_order[4], defer=True)
                        tl2[3]()
                        tl2[4]()

                mark(f'op{l}')
                if l + 1 < DEPTH:
                    prep_dma(l + 1)
                mark(f'n2_{l}')
                GPW = 2 + NP_
                GSW = 6 * 16
                GW = GPW + GSW
                nb0 = 768 + 2048
                g_v, Rg = AR.carve("graw", nb0, 2 * GW, 14)
                tm_v, Rtm = AR.carve("ftmp", nb0 + 2 * GW, 3 * 512, 3)
                fo_v, Rfo = AR.carve("fostg", nb0 + 2 * GW + 1536, 2 * 128, 2)
                ab0 = nb0 + 2 * GW + 1536 + 256
                actb_v, Rab = AR.carve("actb", ab0, 8 * NT // 2, 5)
                actb = actb_v.bitcast(BF16).rearrange("p (f n) -> p f n", f=8)
                tmi = 0
                pend = {"f": None}
                for (f0, nf) in FF_GROUPS:
                    for fl in range(nf):
                        f = f0 + fl
                        sl = wslot()
                        wv = WS[sl][:, 0:2048].rearrange("p (k n) -> p k n", k=8)
                        wdma(sl, wv[:, :, 0:128], Wup[:, :, f * 128:(f + 1) * 128])
                        wdma(sl, wv[:, :, 128:256], Wup[:, :, D_FF + f * 128:D_FF + (f + 1) * 128])
                        gb = f % 2
                        G = g_v[:, gb * GW:(gb + 1) * GW]
                        S.op("dve", lambda e: e.memset(G[:, 0:2], 0.0), writes=[Rg[7 * gb]])
                        S.op("dve", lambda e: e.tensor_copy(G[:, GPW:GPW + 32], FS[:, f, :]), reads=[Rfs], writes=[Rg[7 * gb + 5]])
                        w0 = VT[:, VT_FW + (l * 3 + 0) * NFF + f:VT_FW + (l * 3 + 0) * NFF + f + 1]
                        w1 = VT[:, VT_FW + (l * 3 + 1) * NFF + f:VT_FW + (l * 3 + 1) * NFF + f + 1]
                        w2 = VT[:, VT_FW + (l * 3 + 2) * NFF + f:VT_FW + (l * 3 + 2) * NFF + f + 1]
                        fb = VT[:, VT_FB + l * NFF + f:VT_FB + l * NFF + f + 1]
                        for ti, (c0, n) in enumerate(TTS):
                            pg, Rpg = bank()
                            for k in range(KC):
                                S.op("pe", lambda e: e.matmul(pg[:, 0:n], wv[:, k, 0:128], H[:, k, c0:c0 + n], start=(k == 0), stop=(k == 7)),
                                     reads=[RWS[sl], RH[ti]], writes=[Rpg], inc=(k == 7))
                            pv, Rpv = bank()
                            for k in range(KC):
                                S.op("pe", lambda e: e.matmul(pv[:, 0:n], wv[:, k, 128:256], H[:, k, c0:c0 + n], start=(k == 0), stop=(k == 7)),
                                     reads=[RWS[sl], RH[ti]], writes=[Rpv], inc=(k == 7))
                            if ti < 4:
                                gc, sh = 2 + c0, 1
                            else:
                                gc, sh = GPW + 32, 16
                            gw_ = Rg[7 * gb + 1 + ti] if ti < 4 else Rg[7 * gb + 6]
                            gr_ = [Rg[7 * gb + ti], Rg[7 * gb + 1 + ti]] if ti < 4 else [Rg[7 * gb + 5], Rg[7 * gb + 6]]
                            S.op("act", lambda e: e.copy(G[:, gc:gc + n], pg[:, 0:n]), reads=[Rpg], writes=[gw_])
                            q = tmi % 3
                            tmi += 1
                            T_ = tm_v[:, q * 512:q * 512 + n]
                            S.op("act", lambda e: e.activation(T_, pg[:, 0:n], AF.Identity, bias=fb, scale=w2), reads=[Rpg, Rvt], writes=[Rtm[q]])
                            S.op("dve", lambda e: e.scalar_tensor_tensor(T_, G[:, gc - sh:gc - sh + n], w1, T_, ALU.mult, ALU.add),
                                 reads=gr_ + [Rvt], writes=[Rtm[q]])
                            S.op("dve", lambda e: e.scalar_tensor_tensor(T_, G[:, gc - 2 * sh:gc - 2 * sh + n], w0, T_, ALU.mult, ALU.add),
                                 reads=gr_ + [Rvt], writes=[Rtm[q]])
                            def stage2(T_=T_, q=q, pv=pv, Rpv=Rpv, fl=fl, c0=c0, n=n, ti=ti):
                                S.op("act", lambda e: e.activation(T_, T_, AF.Silu), reads=[Rtm[q]], writes=[Rtm[q]])
                                S.op("dve", lambda e: e.tensor_tensor(actb[:, fl, c0:c0 + n], pv[:, 0:n], T_, ALU.mult),
                                     reads=[Rpv, Rtm[q]], writes=[Rab[ti]])
                            if pend["f"] is not None:
                                pend["f"]()
                            pend["f"] = stage2
                        pt, Rpt = bank()
                        for k in range(KC):
                            S.op("pe", lambda e: e.matmul(pt[0:66, 0:128], H[:, k, NT - 66:NT], wv[:, k, 0:128], start=(k == 0), stop=(k == 7)),
                                 reads=[RWS[sl], RH[3], RH[4]], writes=[Rpt], inc=(k == 7))
                        ob = f % 2
                        S.op("act", lambda e: e.copy(fo_v[0:66, ob * 128:(ob + 1) * 128], pt[0:66, 0:128]), reads=[Rpt], writes=[Rfo[ob]])
                        out_dma(o_ffn_p[l][:, f * 128:(f + 1) * 128], fo_v[0:2, ob * 128:(ob + 1) * 128], [Rfo[ob]], f"fo{ob}")
                        for i in range(2):
                            out_dma(o_ffn_s[l][:, i, f * 128:(f + 1) * 128], fo_v[2 + 16 * (2 + i):18 + 16 * (2 + i), ob * 128:(ob + 1) * 128],
                                    [Rfo[ob]], f"fo{ob}")
                    if pend["f"] is not None:
                        pend["f"]()
                        pend["f"] = None
                    last = (f0 + nf == NFF)
                    if not last:
                        for mm in range(4):
                            sl = wslot()
                            wd = WS[sl][:, 0:nf * 256].rearrange("p (k n) -> p k n", k=nf)
                            wdma(sl, wd, Wdn[:, f0:f0 + nf, mm * 256:(mm + 1) * 256])
                            for m2 in range(2):
                                m = mm * 2 + m2
                                for ti, (c0, n) in enumerate(TTS):
                                    pb, Rpb = bank()
                                    for kk in range(nf):
                                        S.op("pe", lambda e: e.matmul(pb[:, 0:n], wd[:, kk, m2 * 128:(m2 + 1) * 128], actb[:, kk, c0:c0 + n],
                                                                      start=(kk == 0), stop=(kk == nf - 1)),
                                             reads=[RWS[sl], Rab[ti]], writes=[Rpb], inc=(kk == nf - 1))
                                    S.op("dve", lambda e: e.tensor_tensor(X[:, m, c0:c0 + n], pb[:, 0:n], X[:, m, c0:c0 + n], ALU.add),
                                         reads=[Rpb], writes=[RX[m][ti]])
                    else:
                        wparts = []
                        for k0 in range(0, nf, 3):
                            nk = min(3, nf - k0)
                            sl = wslot()
                            wvw = WS[sl][:, 0:nk * 1024].rearrange("p (k n) -> p k n", k=nk)
                            wdma(sl, wvw, Wdn[:, f0 + k0:f0 + k0 + nk, :])
                            wparts.append((sl, wvw, k0))
                        if l + 1 < DEPTH:
                            prep(l + 1)
                        nctx = norm_begin()
                        is_fin = (l + 1 == DEPTH)
                        if is_fin:
                            fcb, Tq = make_final_cb()

                        tails = {}

                        def head(ti):
                            if not is_fin:
                                tails[ti] = norm_tt(nctx, VT_N1 + (l + 1) * 8, ti, defer=True)
                            else:
                                tails[ti] = norm_tt(nctx, VT_FN, ti, final=True, ycb=fcb, defer=True)

                        order = [4, 0, 1, 2, 3]
                        for pos, ti in enumerate(order):
                            c0, n = TTS[ti]
                            for m in range(8):
                                pb, Rpb = bank()
                                for kk in range(nf):
                                    sl, wvw, k0 = wparts[kk // 3]
                                    S.op("pe", lambda e: e.matmul(pb[:, 0:n], wvw[:, kk - k0, m * 128:(m + 1) * 128], actb[:, kk, c0:c0 + n],
                                                                  start=(kk == 0), stop=(kk == nf - 1)),
                                         reads=[RWS[sl], Rab[ti]], writes=[Rpb], inc=(kk == nf - 1))
                                S.op("dve", lambda e: e.tensor_tensor(X[:, m, c0:c0 + n], pb[:, 0:n], X[:, m, c0:c0 + n], ALU.add),
                                     reads=[Rpb], writes=[RX[m][ti]])
                            if pos >= 1:
                                head(order[pos - 1])
                            if is_fin and pos >= 3:
                                Tq[order[pos - 3]]()
                            if pos >= 2:
                                tails[order[pos - 2]]()
                        head(order[4])
                        if is_fin:
                            Tq[order[2]]()
                        tails[order[3]]()
                        if is_fin:
                            Tq[order[3]]()
                        tails[order[4]]()
                        if is_fin:
                            Tq[order[4]]()

            mark('layers')

        except _Stop:
            pass

        sp = S.E["sp"]["eng"]
        for d in DOUT.values():
            if d.cnt:
                sp.wait_ge(d.sem, d.cnt)
        for d in S.dsems:
            assert d.cnt < 60000, d.cnt
        for e in S.E.values():
            assert e["cnt"] < 60000
    return nc


def _consts():
    c = np.zeros((128, C_N), np.float32)
    c[:, C_ID:C_ID + 128] = np.eye(128, dtype=np.float32)
    s = np.arange(128)
    c[:, C_TRI:C_TRI + 128] = (s[:, None] <= s[None, :]).astype(np.float32)
    idx = np.arange(64)
    t_, b_ = idx // 16, idx % 16
    c[0:64, C_M64:C_M64 + 64] = ((b_[:, None] == b_[None, :]) & (t_[:, None] <= t_[None, :])).astype(np.float32)
    c[0:4, C_E4:C_E4 + 64] = (np.arange(4)[:, None] == t_[None, :]).astype(np.float32)
    wins = np.array([[2, 4], [8, 16]], np.float32)
    for j in range(2):
        w = np.where(np.arange(128) < 64, wins[j, 0], wins[j, 1]).astype(np.float32)
        c[:, C_INVW + j] = 1.0 / w
        for t in range(16):
            c[:, C_INVC + 16 * j + t] = 1.0 / np.minimum(t + 1.0, w)
            c[:, C_CT + 16 * j + t] = np.where(t < w, 1.0 / w, 0.0) - (1.0 if t == 0 else 0.0)
    return c


_NC_CACHE = {}


def kernel(x_prompt, x_sample, state_pool, state_conv, state_ffn_conv,
           norm1, w_in, pool_w, pool_scale, sgu_w, sgu_b, conv_w, conv_b,
           cnorm_g, cnorm_b, w_out, norm2, w_up, ffn_conv_w, ffn_conv_b,
           w_down, final_norm):
    f = lambda a: np.ascontiguousarray(np.asarray(a, dtype=np.float32))
    if "nc" not in _NC_CACHE:
        _NC_CACHE["nc"] = build_nc()
    nc = _NC_CACHE["nc"]
    shared = dict(norm1=f(norm1), w_in=f(w_in), pool_w=f(pool_w), pool_scale=f(pool_scale), sgu_w=f(sgu_w),
                  sgu_b=f(sgu_b), conv_w=f(conv_w), conv_b=f(conv_b), cnorm_g=f(cnorm_g), cnorm_b=f(cnorm_b),
                  w_out=f(w_out), norm2=f(norm2), w_up=f(w_up), ffn_conv_w=f(ffn_conv_w), ffn_conv_b=f(ffn_conv_b),
                  w_down=f(w_down), final_norm=f(final_norm), cst=_consts())
    x_prompt, x_sample = f(x_prompt), f(x_sample)
    state_pool, state_conv, state_ffn_conv = f(state_pool), f(state_conv), f(state_ffn_conv)
    in_maps = []
    for c in range(8):
        m = dict(shared)
        m["xp"] = x_prompt[c]
        m["xs"] = x_sample[16 * c:16 * c + 16]
        m["st_pool"] = np.ascontiguousarray(state_pool[:, 16 * c:16 * c + 16])
        m["st_conv"] = np.ascontiguousarray(state_conv[:, 16 * c:16 * c + 16])
        m["st_ffn"] = np.ascontiguousarray(state_ffn_conv[:, 16 * c:16 * c + 16])
        in_maps.append(m)
    res = run_bass_kernel_spmd(nc, in_maps, core_ids=list(range(8)))
    R = res.results
    cat0 = lambda k: np.stack([R[c][k] for c in range(8)], axis=0)
    y_p = cat0("y_p")
    y_s = np.concatenate([R[c]["y_s"] for c in range(8)], axis=0)
    pool_p = np.stack([R[c]["o_pool_p"] for c in range(8)], axis=1)
    pool_s = np.concatenate([R[c]["o_pool_s"] for c in range(8)], axis=1)
    conv_p = np.stack([R[c]["o_conv_p"] for c in range(8)], axis=1)
    conv_s = np.concatenate([R[c]["o_conv_s"] for c in range(8)], axis=1)
    ffn_p = np.stack([R[c]["o_ffn_p"] for c in range(8)], axis=1)
    ffn_s = np.concatenate([R[c]["o_ffn_s"] for c in range(8)], axis=1)
    v_s = np.concatenate([R[c]["o_v_s"] for c in range(8)], axis=1)
    return (y_p, y_s, pool_p, pool_s, conv_p, conv_s, ffn_p, ffn_s, v_s)
```

```python
import numpy as np
from contextlib import ExitStack
import concourse.bass as bass
import concourse.mybir as mybir
from concourse.bass_utils import run_bass_kernel_spmd

F32 = mybir.dt.float32
BF16 = mybir.dt.bfloat16
AF = mybir.ActivationFunctionType
ALU = mybir.AluOpType

DEPTH = 4
D = 1024
KC = 8
NP_ = 2048
NS = 64
NT = NP_ + NS
TTS = [(0, 512), (512, 512), (1024, 512), (1536, 512), (2048, 64)]
D_A, D_B, D_C = 256, 384, 384
D_FF = 2816
NFF = 22
D_IN = 1792
EPS = 1e-6
FF_GROUPS = [(0, 8), (8, 8), (16, 6)]

VT_N1 = 0
VT_N2 = VT_N1 + 32
VT_FN = VT_N2 + 32
VT_PS = VT_FN + 8
VT_CW = VT_PS + 8
VT_CB = VT_CW + 372
VT_CG = VT_CB + 12
VT_CBB = VT_CG + 12
VT_FW = VT_CBB + 12
VT_FB = VT_FW + 264
VT_N = VT_FB + 88

C_ID = 0
C_TRI = 128
C_M64 = 256
C_E4 = 320
C_INVW = 384
C_INVC = 386
C_CT = 418
C_N = 450


class Res:
    __slots__ = ("name", "w", "r", "excl")

    def __init__(self, name, inherit=None, excl=False):
        self.name = name
        self.w = None
        self.r = dict(inherit) if inherit else {}
        self.excl = excl


class DSem:
    __slots__ = ("sem", "cnt")

    def __init__(self, sem):
        self.sem = sem
        self.cnt = 0


class Sched:
    def __init__(self, nc, stack):
        self.nc = nc
        self.stack = stack
        self.E = {}
        for name, eng in [("pe", nc.tensor), ("act", nc.scalar), ("dve", nc.vector),
                          ("pool", nc.gpsimd), ("sp", nc.sync)]:
            sem = stack.enter_context(nc.semaphore("s_" + name))
            self.E[name] = {"eng": eng, "sem": sem, "cnt": 0, "waited": {}}
        self.dsems = []

    def dsem(self, name):
        d = DSem(self.stack.enter_context(self.nc.semaphore(name)))
        self.dsems.append(d)
        return d

    def _deps(self, reads, writes):
        deps = {}
        for r in reads:
            if r.w is not None and deps.get(r.w[0], 0) < r.w[1]:
                deps[r.w[0]] = r.w[1]
            if r.excl:
                for s, v in r.r.items():
                    if deps.get(s, 0) < v:
                        deps[s] = v
        for w in writes:
            if w.w is not None and deps.get(w.w[0], 0) < w.w[1]:
                deps[w.w[0]] = w.w[1]
            for s, v in w.r.items():
                if deps.get(s, 0) < v:
                    deps[s] = v
        return deps

    def _emit_waits(self, ename, deps):
        st = self.E[ename]
        for s, v in deps.items():
            if s == st["sem"] and (v > st["cnt"] or ename == "pe"):
                continue
            if st["waited"].get(s, 0) < v:
                for oe in self.E.values():
                    if oe["sem"] == s:
                        assert v <= oe["cnt"], ("wait on deferred event", ename, v, oe["cnt"])
                st["eng"].wait_ge(s, v)
                st["waited"][s] = v

    def _record(self, ev, reads, writes):
        s, v = ev
        for r in reads:
            if r.r.get(s, 0) < v:
                r.r[s] = v
        for w in writes:
            w.w = ev
            w.r = {}

    def op(self, ename, fn, reads=(), writes=(), inc=True):
        st = self.E[ename]
        self._emit_waits(ename, self._deps(reads, writes))
        ins = fn(st["eng"])
        ev = (st["sem"], st["cnt"] + 1)
        if inc:
            ins.then_inc(st["sem"], 1)
            st["cnt"] += 1
        self._record(ev, reads, writes)
        return ins

    def dma(self, qname, out, in_, ds, reads=(), writes=(), **kw):
        st = self.E[qname]
        deps = self._deps(reads, writes)
        deps.pop(ds.sem, None)
        self._emit_waits(qname, deps)
        ins = st["eng"].dma_start(out=out, in_=in_, **kw)
        ds.cnt += 16
        ins.then_inc(ds.sem, 16)
        self._record((ds.sem, ds.cnt), reads, writes)
        return ins

    def wait_all(self, ename, ress):
        deps = {}
        for r in ress:
            if r.w is not None and deps.get(r.w[0], 0) < r.w[1]:
                deps[r.w[0]] = r.w[1]
            for s, v in r.r.items():
                if deps.get(s, 0) < v:
                    deps[s] = v
        self._emit_waits(ename, deps)


class Arena:
    def __init__(self, ap, nwords):
        self.ap = ap
        self.n = nwords
        self.live = []

    def carve(self, name, start, nwords, nres=1):
        end = start + nwords
        assert 0 <= start and end <= self.n, (name, start, end, self.n)
        inh = {}
        keep = []
        for (a, b, rs) in self.live:
            if a < end and start < b:
                for r in rs:
                    if r.w is not None and inh.get(r.w[0], 0) < r.w[1]:
                        inh[r.w[0]] = r.w[1]
                    for s, v in r.r.items():
                        if inh.get(s, 0) < v:
                            inh[s] = v
                if not (start <= a and b <= end):
                    keep.append((a, b, rs))
            else:
                keep.append((a, b, rs))
        rs = [Res(f"{name}{i}", inh) for i in range(nres)]
        keep.append((start, end, rs))
        self.live = keep
        return self.ap[:, start:end], rs


class _Stop(Exception):
    pass


def build_nc(stop=None):
    nc = bass.Bass("TRN2", target_bir_lowering=False)

    def mark(k):
        if stop is not None and k == stop:
            raise _Stop()

    def din(name, shape):
        return nc.dram_tensor(name, list(shape), F32, kind="ExternalInput").ap()

    def dout(name, shape):
        return nc.dram_tensor(name, list(shape), F32, kind="ExternalOutput").ap()

    xp = din("xp", [NP_, D])
    xs = din("xs", [16, 4, D])
    sp_ = din("st_pool", [DEPTH, 16, 15, D_A])
    sc_ = din("st_conv", [DEPTH, 16, 30, D_C])
    sf_ = din("st_ffn", [DEPTH, 16, 2, D_FF])
    norm1 = din("norm1", [DEPTH, D])
    w_in = din("w_in", [DEPTH, D, D_IN])
    pool_w = din("pool_w", [DEPTH, 4, 64, 64])
    pool_scale = din("pool_scale", [DEPTH, D_A])
    sgu_w = din("sgu_w", [DEPTH, 4, 128, 128])
    sgu_b = din("sgu_b", [DEPTH, 4, 128])
    conv_w = din("conv_w", [DEPTH, 31, D_C])
    conv_b = din("conv_b", [DEPTH, D_C])
    cnorm_g = din("cnorm_g", [DEPTH, D_C])
    cnorm_b = din("cnorm_b", [DEPTH, D_C])
    w_out = din("w_out", [DEPTH, D, D])
    norm2 = din("norm2", [DEPTH, D])
    w_up = din("w_up", [DEPTH, D, 2 * D_FF])
    ffn_conv_w = din("ffn_conv_w", [DEPTH, 3, D_FF])
    ffn_conv_b = din("ffn_conv_b", [DEPTH, D_FF])
    w_down = din("w_down", [DEPTH, D_FF, D])
    final_norm = din("final_norm", [D])
    cst_d = din("cst", [128, C_N])

    y_p = dout("y_p", [NP_, D])
    y_s = dout("y_s", [16, 4, D])
    o_pool_p = dout("o_pool_p", [DEPTH, 15, D_A])
    o_pool_s = dout("o_pool_s", [DEPTH, 16, 15, D_A])
    o_conv_p = dout("o_conv_p", [DEPTH, 30, D_C])
    o_conv_s = dout("o_conv_s", [DEPTH, 16, 30, D_C])
    o_ffn_p = dout("o_ffn_p", [DEPTH, 2, D_FF])
    o_ffn_s = dout("o_ffn_s", [DEPTH, 16, 2, D_FF])
    o_v_s = dout("o_v_s", [DEPTH, 16, 4, D_B])

    with ExitStack() as st:
        S = Sched(nc, st)

        def sb(name, shape, dt):
            return st.enter_context(nc.sbuf_tensor(name, shape, dt))

        X = sb("X", [128, KC, NT], F32)
        H = sb("H", [128, KC, NT], BF16)
        NSLOT = 3
        WSA = sb("wsall", [128, NSLOT * 3072], BF16)
        WS = [WSA[:, i * 3072:(i + 1) * 3072] for i in range(NSLOT)]
        CST = sb("cstt", [128, C_N], F32)
        VT = sb("vt", [128, VT_N], F32)
        PW = sb("pw", [128, 2, 128], BF16)
        PWf = sb("pwf", [128, 2, 128], F32)
        ONES = sb("ones", [128, 128], BF16)
        IDB = sb("idb", [128, 128], BF16)
        EPST = sb("epst", [128, 1], F32)
        WT = sb("wt", [128, 4, 128], BF16)
        WTS = sb("wts", [64, 4, 64], BF16)
        W4 = sb("w4", [4, 4, 4], F32)
        BB = sb("bb", [96, 4, 192], F32)
        M1 = sb("m1", [4, 64], F32)
        STG = [sb(f"stg{i}", [128, 384], F32) for i in range(2)]
        FS = sb("fs", [128, NFF, 32], F32)
        WST = [sb(f"wst{i}", [128, 128], F32) for i in range(4)]
        AW = (nc.sbuf_bytes_remaining // 4) - 16
        ARENA_T = sb("arena", [128, AW], F32)
        AR = Arena(ARENA_T, AW)

        PS = [st.enter_context(nc.psum_tensor(f"ps{i}", [128, 512], F32)) for i in range(8)]
        RPS = [Res(f"ps{i}", excl=True) for i in range(8)]
        pstate = {"i": 0}

        reserved = set()

        def bank():
            while True:
                i = pstate["i"] % 8
                pstate["i"] += 1
                if i not in reserved:
                    pstate["last"] = i
                    return PS[i], RPS[i]

        ident = CST[:, C_ID:C_ID + 128]
        trimask = CST[:, C_TRI:C_TRI + 128]
        mask64 = CST[0:64, C_M64:C_M64 + 64]
        E4 = CST[0:4, C_E4:C_E4 + 64]

        RX = [[Res(f"x{k}_{t}") for t in range(5)] for k in range(KC)]
        RH = [Res(f"h{t}") for t in range(5)]
        RWS = [Res(f"ws{i}") for i in range(NSLOT)]
        DWS = [S.dsem(f"dws{i}") for i in range(NSLOT)]
        DWB = S.dsem("dwb")
        Rcst, Rvt, Rpw, Rpwf, Rones, Rwt, Rwts, Rbr, Rbrs, Rw4, Rm1, Rfs, Rbb = [
            Res(n) for n in "cst vt pw pwf ones wt wts br brs w4 m1 fs bb".split()]
        RSTG = [Res("stg0"), Res("stg1")]
        DSTG = [S.dsem("dstg0"), S.dsem("dstg1")]
        RWST = [Res(f"wst{i}") for i in range(4)]
        DWST = [S.dsem(f"dwst{i}") for i in range(4)]
        DM = {k: S.dsem("dm_" + k) for k in ["cst", "w4", "br", "brs", "pwf", "fst"]}
        ROUT = Res("out")
        DOUT = {k: S.dsem("do_" + k) for k in ["d2d", "cstg", "pstg", "vstg", "fo0", "fo1", "yt0", "yt1"]}

        def out_dma(dst, src, reads, key):
            S.dma("sp", dst, src, DOUT[key], reads=reads)

        wstate = {"i": 0}

        def wslot():
            i = wstate["i"] % NSLOT
            wstate["i"] += 1
            return i

        def wbig(dst, src):
            S.dma("pool", dst, src, DWB, writes=RWS)

        def wdma(slot, dst, src):
            S.dma("pool", dst, src, DWS[slot], writes=[RWS[slot]])

        try:
            S.dma("sp", CST[:], cst_d, DM["cst"], writes=[Rcst])
            S.op("dve", lambda e: e.memset(ONES[:], 1.0), writes=[Rones])
            S.op("dve", lambda e: e.memset(EPST[:], EPS), writes=[Rones])
            S.op("dve", lambda e: e.tensor_copy(IDB[:], CST[:, C_ID:C_ID + 128]), reads=[Rcst], writes=[Rones])

            xs_v, Rxs = AR.carve("xstage", 2816, 2 * 4096, 2)
            xst = [xs_v[:, 0:4096].rearrange("p (a d) -> p a d", a=4), xs_v[:, 4096:8192].rearrange("p (a d) -> p a d", a=4)]
            dxs = [S.dsem("dxs0"), S.dsem("dxs1")]
            for ti in range(2):
                c0 = TTS[ti][0]
                S.dma("sp", xst[ti], xp[c0:c0 + 512, :].rearrange("(a p) d -> p a d", p=128), dxs[ti], writes=[Rxs[ti]])

            vt_src = [
                (norm1.rearrange("l (k p) -> (l k) p", p=128), VT_N1),
                (norm2.rearrange("l (k p) -> (l k) p", p=128), VT_N2),
                (final_norm.rearrange("(k p) -> k p", p=128), VT_FN),
                (pool_scale.rearrange("l (k p) -> (l k) p", p=128), VT_PS),
                (conv_w.rearrange("l i (k p) -> (l i k) p", p=128), VT_CW),
                (conv_b.rearrange("l (k p) -> (l k) p", p=128), VT_CB),
                (cnorm_g.rearrange("l (k p) -> (l k) p", p=128), VT_CG),
                (cnorm_b.rearrange("l (k p) -> (l k) p", p=128), VT_CBB),
                (ffn_conv_w.rearrange("l i (k p) -> (l i k) p", p=128), VT_FW),
                (ffn_conv_b.rearrange("l (k p) -> (l k) p", p=128), VT_FB),
            ]
            bi = 0
            for src, off in vt_src:
                R = src.shape[0]
                for r0 in range(0, R, 128):
                    r = min(128, R - r0)
                    s_ = bi % 4
                    bi += 1
                    S.dma("sp", WST[s_][0:r, :], src[r0:r0 + r, :], DWST[s_], writes=[RWST[s_]])
                    pb, Rpb = bank()
                    S.op("pe", lambda e: e.transpose(pb[:, 0:r], WST[s_][0:r, :], ident[0:r, 0:r]),
                         reads=[RWST[s_], Rcst], writes=[Rpb])
                    S.op("act", lambda e: e.copy(VT[:, off + r0:off + r0 + r], pb[:, 0:r]), reads=[Rpb], writes=[Rvt])

            mark('xload')
            def norm_begin():
                sq_v, Rsq = AR.carve("sq", 0, 3 * 256, 3)
                rt_v, Rrt = AR.carve("rt", 768, 2 * 512, 2)
                ri_v, Rri = AR.carve("ri", 768 + 1024, 2 * 512, 2)
                return (sq_v.bitcast(BF16).rearrange("p (a n) -> p a n", a=3), Rsq, rt_v, Rrt, ri_v, Rri)

            def norm_tt(ctx, gcol, ti, final=False, ycb=None, defer=False):
                sqb, Rsq, rt_v, Rrt, ri_v, Rri = ctx
                c0, n = TTS[ti]
                pb, Rpb = bank()
                bidx = pstate["last"]
                if defer:
                    reserved.add(bidx)
                for kc in range(KC):
                    q = kc % 3
                    S.op("act", lambda e: e.activation(sqb[:, q, 0:n], X[:, kc, c0:c0 + n], AF.Square),
                         reads=[RX[kc][ti]], writes=[Rsq[q]])
                    S.op("pe", lambda e: e.matmul(pb[:, 0:n], ONES[:], sqb[:, q, 0:n], start=(kc == 0), stop=(kc == KC - 1)),
                         reads=[Rsq[q], Rones], writes=[Rpb])
                b = ti % 2
                rt = rt_v[:, b * 512:b * 512 + n]
                ri = ri_v[:, b * 512:b * 512 + n]

                def tail():
                    S.op("act", lambda e: e.activation(rt, pb[:, 0:n], AF.Ln, bias=EPST[:, 0:1], scale=1.0 / D),
                         reads=[Rpb, Rones], writes=[Rrt[b]])
                    S.op("act", lambda e: e.activation(ri, rt, AF.Exp, scale=-0.5), reads=[Rrt[b]], writes=[Rri[b]])
                    if not final:
                        for kc in range(KC):
                            S.op("dve", lambda e: e.scalar_tensor_tensor(H[:, kc, c0:c0 + n], X[:, kc, c0:c0 + n],
                                                                         VT[:, gcol + kc:gcol + kc + 1], ri, ALU.mult, ALU.mult),
                                 reads=[RX[kc][ti], Rri[b], Rvt], writes=[RH[ti]])
                    else:
                        ycb(ti, c0, n, ri, Rri[b])
                    reserved.discard(bidx)
                if defer:
                    return tail
                tail()

            def rmsnorm(gcol, final=False, ycb=None):
                ctx = norm_begin()
                for ti in range(5):
                    norm_tt(ctx, gcol, ti, final, ycb)

            def prep_dma(l):
                for h in range(4):
                    S.dma("sp", WST[h][:], sgu_w[l, h], DWST[h], writes=[RWST[h]])
                S.dma("sp", W4[:], sgu_w[l, :, 0:4, 0:4].rearrange("h a b -> a h b"), DM["w4"], writes=[Rw4])
                S.dma("sp", BB[:, :, 0:128], sgu_b[l].partition_broadcast(96), DM["br"], writes=[Rbb])

            def prep(l):
                for h in range(4):
                    s_ = h
                    pb, Rpb = bank()
                    S.op("pe", lambda e: e.transpose(pb[:, 0:128], WST[s_][:], ident), reads=[RWST[s_], Rcst], writes=[Rpb])
                    S.op("dve", lambda e: e.tensor_tensor(WT[:, h, :], pb[:, 0:128], trimask, ALU.mult),
                         reads=[Rpb, Rcst], writes=[Rwt])
                for h in range(4):
                    S.op("dve", lambda e: e.tensor_copy(BB[:, h, 128:192].rearrange("p (t b) -> p t b", b=16),
                                                        BB[:, h, 0:4].unsqueeze(2).broadcast_to([96, 4, 16])),
                         reads=[Rbb], writes=[Rbb])
                for h in range(4):
                    pb, Rpb = bank()
                    S.op("pe", lambda e: e.matmul(pb[0:4, 0:64], W4[:, h, :], E4, start=True, stop=True),
                         reads=[Rw4, Rcst], writes=[Rpb])
                    S.op("act", lambda e: e.copy(M1[:], pb[0:4, 0:64]), reads=[Rpb], writes=[Rm1])
                    pb2, Rpb2 = bank()
                    S.op("pe", lambda e: e.matmul(pb2[0:64, 0:64], E4, M1[:], start=True, stop=True),
                         reads=[Rm1, Rcst], writes=[Rpb2])
                    S.op("dve", lambda e: e.tensor_tensor(WTS[:, h, :], pb2[0:64, 0:64], mask64, ALU.mult),
                         reads=[Rpb2, Rcst], writes=[Rwts])
                S.op("dve", lambda e: e.memset(PWf[:], 0.0), writes=[Rpwf])
                for g in range(4):
                    j, q = g // 2, g % 2
                    S.dma("sp", PWf[64 * q:64 * q + 64, j, 64 * q:64 * q + 64], pool_w[l, g], DM["pwf"], writes=[Rpwf])
                S.op("dve", lambda e: e.tensor_copy(PW[:], PWf[:]), reads=[Rpwf], writes=[Rpw])


            def make_final_cb():
                yf_v, Ryf = AR.carve("yf", 2816, 4096, 1)
                yt0_v, Ryt0 = AR.carve("yt0", 2816 + 4096, 1024, 1)
                yt1_v, Ryt1 = AR.carve("yt1", 8900 + 6 * NT // 2, 1024, 1)
                yts = [yt0_v, yt1_v]
                Ryt = [Ryt0[0], Ryt1[0]]
                ycnt = {"i": 0}

                def final_cb(ti, c0, n, ri, Rri_):
                    yb = 0
                    Y = yf_v[:, 0:4096].rearrange("p (k n) -> p k n", k=8)
                    for kc in range(KC):
                        S.op("dve", lambda e: e.scalar_tensor_tensor(Y[:, kc, 0:n], X[:, kc, c0:c0 + n], VT[:, VT_FN + kc:VT_FN + kc + 1], ri,
                                                                     ALU.mult, ALU.mult),
                             reads=[RX[kc][ti], Rri_, Rvt], writes=[Ryf[yb]])
                    def T_stage(ti=ti, c0=c0, n=n):
                        nsub = 4 if ti < 4 else 1
                        rows = 128 if ti < 4 else 64
                        for a in range(nsub):
                            q = ycnt["i"] % 2
                            ycnt["i"] += 1
                            YT = yts[q]
                            for half in range(2):
                                pb, Rpb = bank()
                                for kq in range(4):
                                    kc = half * 4 + kq
                                    S.op("pe", lambda e: e.transpose(pb[0:rows, kq * 128:(kq + 1) * 128], Y[:, kc, a * 128:a * 128 + rows], ident),
                                         reads=[Ryf[yb], Rcst], writes=[Rpb], inc=(kq == 3))
                                if half == 0:
                                    S.op("act", lambda e: e.copy(YT[0:rows, 0:512], pb[0:rows, :]), reads=[Rpb], writes=[Ryt[q]])
                                else:
                                    S.op("dve", lambda e: e.tensor_copy(YT[0:rows, 512:1024], pb[0:rows, :]), reads=[Rpb], writes=[Ryt[q]])
                            if ti < 4:
                                out_dma(y_p[c0 + a * 128:c0 + (a + 1) * 128, :], YT[:, :], [Ryt[q]], f"yt{q}")
                            else:
                                for t in range(4):
                                    out_dma(y_s[:, t, :], YT[16 * t:16 * t + 16, :], [Ryt[q]], f"yt{q}")
                    Tq[ti] = T_stage
                Tq = {}
                return final_cb, Tq

            nctx0 = norm_begin()
            for ti, (c0, n) in enumerate(TTS):
                s_ = ti % 2
                if ti < 4:
                    if ti >= 2:
                        S.dma("sp", xst[s_], xp[c0:c0 + 512, :].rearrange("(a p) d -> p a d", p=128), dxs[s_], writes=[Rxs[s_]])
                    na, rows = 4, 128
                else:
                    for t in range(4):
                        S.dma("sp", xst[s_][16 * t:16 * t + 16, 0, :], xs[:, t, :], dxs[s_], writes=[Rxs[s_]])
                    na, rows = 1, 64
                for kc in range(KC):
                    pb, Rpb = bank()
                    for a in range(na):
                        S.op("pe", lambda e: e.transpose(pb[:, a * 128:a * 128 + rows], xst[s_][0:rows, a, kc * 128:(kc + 1) * 128],
                                                         ident[0:rows, 0:rows]),
                             reads=[Rxs[s_], Rcst], writes=[Rpb], inc=(a == na - 1))
                    eng = "act" if kc % 2 == 0 else "dve"
                    if eng == "act":
                        S.op("act", lambda e: e.copy(X[:, kc, c0:c0 + n], pb[:, 0:n]), reads=[Rpb], writes=[RX[kc][ti]])
                    else:
                        S.op("dve", lambda e: e.tensor_copy(X[:, kc, c0:c0 + n], pb[:, 0:n]), reads=[Rpb], writes=[RX[kc][ti]])
                if ti >= 1:
                    norm_tt(nctx0, VT_N1, ti - 1)
            norm_tt(nctx0, VT_N1, 4)

            for l in range(DEPTH):
                Win = w_in[l].rearrange("(k p) n -> p k n", p=128)
                Wout = w_out[l]
                Wup = w_up[l].rearrange("(k p) n -> p k n", p=128)
                Wdn = w_down[l].rearrange("(k p) n -> p k n", p=128)

                if l == 0:
                    prep_dma(0)
                    prep(0)
                mark(f'prep{l}')

                mark(f'n1_{l}')
                CPW = 30 + NP_
                CSW = 34 * 16
                CW_ = CPW + CSW
                BB_ = AW - 9504
                AB_ = AW - 5280
                CB_ = AW - 3168
                cin_v, Rcin = AR.carve("cin", 0, 3 * CW_ // 2, 3 * 6)
                cin = cin_v.bitcast(BF16).rearrange("p (j n) -> p j n", j=3)
                d0_ = 3 * CW_ // 2
                dg_v, Rdg = AR.carve("dg", d0_, 3 * 31 * 64, 3)
                dgb = dg_v.bitcast(BF16).rearrange("p (b i n) -> p b i n", b=3, i=31)
                a1 = d0_ + 3 * 31 * 64
                sig_v, Rsig = AR.carve("sig", a1, 2 * 512, 2)
                a2 = a1 + 1024
                cst_v, Rcstg = AR.carve("cstg", a2, 384, 1)

                def RC(j, k):
                    return Rcin[j * 6 + k]

                for j in range(3):
                    S.op("dve", lambda e: e.memset(cin[:, j, 0:30], 0.0), writes=[RC(j, 0)])
                def cst_dma(q):
                    S.dma("sp", STG[q % 2][0:120, 0:384], sc_[l, 4 * q:4 * q + 4].rearrange("b i c -> (b i) c"), DSTG[q % 2], writes=[RSTG[q % 2]])

                def cst_tr(q):
                    s_ = q % 2
                    for j2 in range(3):
                        pb, Rpb = bank()
                        S.op("pe", lambda e: e.transpose(pb[:, 0:120], STG[s_][0:120, j2 * 128:(j2 + 1) * 128], ident[0:120, 0:120]),
                             reads=[RSTG[s_], Rcst], writes=[Rpb])
                        dst = cin[:, j2, CPW:CPW + 480].rearrange("p (i b) -> p i b", b=16)[:, :, 4 * q:4 * q + 4]
                        srcv = pb[:, 0:120].rearrange("p (b i) -> p i b", i=30)
                        S.op("act", lambda e: e.copy(dst, srcv), reads=[Rpb], writes=[RC(j2, 5)])
                cst_dma(0)
                cst_dma(1)
                cslabs = []
                for j in range(3):
                    sl = wslot()
                    wv = WS[sl][:, 0:2048].rearrange("p (k n) -> p k n", k=8)
                    wdma(sl, wv[:, :, 0:128], Win[:, :, 1024 + 128 * j:1024 + 128 * j + 128])
                    wdma(sl, wv[:, :, 128:256], Win[:, :, 1408 + 128 * j:1408 + 128 * j + 128])
                    cslabs.append((sl, wv))
                for j in range(3):
                    for i in range(31):
                        wcol = VT[:, VT_CW + (l * 31 + i) * 3 + j:VT_CW + (l * 31 + i) * 3 + j + 1]
                        S.op("pool", lambda e: e.tensor_scalar(dgb[:, j, i, :], IDB[:], wcol, 0.0, ALU.mult, ALU.add),
                             reads=[Rvt, Rones], writes=[Rdg[j]])
                out_dma(o_conv_s[l][:, 0:26, :], sc_[l][:, 4:30, :], [], "d2d")
                for j in range(3):
                    sl, wv = cslabs[j]
                    for ti, (c0, n) in enumerate(TTS):
                        pg, Rpg = bank()
                        for k in range(KC):
                            S.op("pe", lambda e: e.matmul(pg[:, 0:n], wv[:, k, 128:256], H[:, k, c0:c0 + n], start=(k == 0), stop=(k == 7)),
                                 reads=[RWS[sl], RH[ti]], writes=[Rpg], inc=(k == 7))
                        pv, Rpv = bank()
                        for k in range(KC):
                            S.op("pe", lambda e: e.matmul(pv[:, 0:n], wv[:, k, 0:128], H[:, k, c0:c0 + n], start=(k == 0), stop=(k == 7)),
                                 reads=[RWS[sl], RH[ti]], writes=[Rpv], inc=(k == 7))
                        b = ti % 2
                        sg = sig_v[:, b * 512:b * 512 + n]
                        S.op("act", lambda e: e.activation(sg, pg[:, 0:n], AF.Sigmoid), reads=[Rpg], writes=[Rsig[b]])
                        if ti < 4:
                            dst = cin[:, j, 30 + c0:30 + c0 + n]
                        else:
                            dst = cin[:, j, CPW + 480:CPW + 544]
                        S.op("dve", lambda e: e.tensor_tensor(dst, pv[:, 0:n], sg, ALU.mult), reads=[Rpv, Rsig[b]], writes=[RC(j, 1 + ti)])
                    if j == 0:
                        cst_tr(0)
                        cst_tr(1)
                        cst_dma(2)
                        cst_dma(3)
                    elif j == 1:
                        cst_tr(2)
                        cst_tr(3)
                    pt, Rpt = bank()
                    for k in range(KC):
                        S.op("pe", lambda e: e.matmul(pt[0:94, 0:256], H[:, k, NT - 94:NT], wv[:, k, 0:256], start=(k == 0), stop=(k == 7)),
                             reads=[RWS[sl], RH[3], RH[4]], writes=[Rpt], inc=(k == 7))
                    sg = sig_v[0:94, 0:128]
                    S.op("act", lambda e: e.activation(sg, pt[0:94, 128:256], AF.Sigmoid), reads=[Rpt], writes=[Rsig[0]])
                    S.op("dve", lambda e: e.tensor_tensor(cst_v[0:94, j * 128:(j + 1) * 128], pt[0:94, 0:128], sg, ALU.mult),
                         reads=[Rpt, Rsig[0]], writes=[Rcstg[0]])
                out_dma(o_conv_p[l], cst_v[0:30, 0:384], [Rcstg[0]], "cstg")
                for t in range(4):
                    out_dma(o_conv_s[l][:, 26 + t, :], cst_v[30 + 16 * t:46 + 16 * t, 0:384], [Rcstg[0]], "cstg")

                mark(f'cin{l}')
                mark(f'conv{l}')
                ln0_ = a1
                ybf_v, Rybf = AR.carve("ybf", ln0_, 4 * 256, 4)
                ybf = ybf_v.bitcast(BF16).rearrange("p (a n) -> p a n", a=4)
                lt_v, Rlt = AR.carve("lnt", ln0_ + 1024, 4 * 512, 4)
                yt_v2, Ryt2 = AR.carve("lnyt", ln0_ + 1024 + 2048, 2 * 512, 2)
                assert ln0_ + 1024 + 2048 + 1024 <= CB_, (ln0_, CB_)
                cout_v, Rco = AR.carve("cout", CB_, 3 * NT // 2, 5)
                cout = cout_v.bitcast(BF16).rearrange("p (j n) -> p j n", j=3)
                for pos, ti in enumerate([4, 0, 1, 2, 3]):
                    c0, n = TTS[ti]
                    pjs = []
                    for j in range(3):
                        pb, Rpb = bank()
                        pjs.append((pb, Rpb))
                        for i in range(31):
                            if ti < 4:
                                src = cin[:, j, i + c0:i + c0 + n]
                                rr = [RC(j, k) for k in range(5)]
                            else:
                                src = cin[:, j, CPW + 16 * i:CPW + 16 * i + 64]
                                rr = [RC(j, 5)]
                            S.op("pe", lambda e: e.matmul(pb[:, 0:n], dgb[:, j, i, :], src, start=(i == 0), stop=(i == 30)),
                                 reads=rr + [Rdg[j]], writes=[Rpb], inc=(i == 30))
                    pm, Rpm = bank()
                    pq, Rpq = bank()
                    for j in range(3):
                        pb, Rpb = pjs[j]
                        cbias = VT[:, VT_CB + l * 3 + j:VT_CB + l * 3 + j + 1]
                        q0, q1 = (2 * j) % 4, (2 * j + 1) % 4
                        S.op("act", lambda e: e.activation(ybf[:, q0, 0:n], pb[:, 0:n], AF.Identity, bias=cbias), reads=[Rpb, Rvt], writes=[Rybf[q0]])
                        S.op("pe", lambda e: e.matmul(pm[:, 0:n], ONES[:], ybf[:, q0, 0:n], start=(j == 0), stop=(j == 2)),
                             reads=[Rybf[q0], Rones], writes=[Rpm])
                        S.op("act", lambda e: e.activation(ybf[:, q1, 0:n], pb[:, 0:n], AF.Square, bias=cbias), reads=[Rpb, Rvt], writes=[Rybf[q1]])
                        S.op("pe", lambda e: e.matmul(pq[:, 0:n], ONES[:], ybf[:, q1, 0:n], start=(j == 0), stop=(j == 2)),
                             reads=[Rybf[q1], Rones], writes=[Rpq])
                    b = (pos % 2) * 2
                    mu = lt_v[:, b * 512:b * 512 + n]
                    rs = lt_v[:, (b + 1) * 512:(b + 1) * 512 + n]
                    S.op("act", lambda e: e.mul(mu, pm[:, 0:n], 1.0 / D_C), reads=[Rpm], writes=[Rlt[b]])
                    S.op("dve", lambda e: e.tensor_tensor(rs, mu, mu, ALU.mult), reads=[Rlt[b]], writes=[Rlt[b + 1]])
                    S.op("dve", lambda e: e.scalar_tensor_tensor(rs, pq[:, 0:n], 1.0 / D_C, rs, ALU.mult, ALU.subtract),
                         reads=[Rpq, Rlt[b + 1]], writes=[Rlt[b + 1]])
                    S.op("dve", lambda e: e.tensor_scalar(rs, rs, 0.0, None, ALU.max), reads=[Rlt[b + 1]], writes=[Rlt[b + 1]])
                    S.op("act", lambda e: e.activation(rs, rs, AF.Ln, bias=EPST[:, 0:1], scale=1.0), reads=[Rlt[b + 1], Rones], writes=[Rlt[b + 1]])
                    S.op("act", lambda e: e.activation(rs, rs, AF.Exp, scale=-0.5), reads=[Rlt[b + 1]], writes=[Rlt[b + 1]])
                    for j in range(3):
                        pb, Rpb = pjs[j]
                        cbias = VT[:, VT_CB + l * 3 + j:VT_CB + l * 3 + j + 1]
                        yq = (ti * 3 + j) % 2
                        a_ = yt_v2[:, yq * 512:yq * 512 + n]
                        S.op("dve", lambda e: e.scalar_tensor_tensor(a_, pb[:, 0:n], cbias, mu, ALU.add, ALU.subtract),
                             reads=[Rpb, Rlt[b], Rvt], writes=[Ryt2[yq]])
                        S.op("dve", lambda e: e.tensor_tensor(a_, a_, rs, ALU.mult), reads=[Rlt[b + 1]], writes=[Ryt2[yq]])
                        gcol = VT[:, VT_CG + l * 3 + j:VT_CG + l * 3 + j + 1]
                        bcol = VT[:, VT_CBB + l * 3 + j:VT_CBB + l * 3 + j + 1]
                        S.op("act", lambda e: e.activation(cout[:, j, c0:c0 + n], a_, AF.Silu, bias=bcol, scale=gcol),
                             reads=[Ryt2[yq], Rvt], writes=[Rco[ti]])

                mark(f'ln{l}')
                PPW = 15 + NP_
                PSW = 19 * 16
                ZW = PPW + PSW
                aout_v, Rao = AR.carve("aout", AB_, NT, 5)
                aout = aout_v.bitcast(BF16).rearrange("p (j n) -> p j n", j=2)
                b1 = 0
                zab_v, Rzab = AR.carve("zab", b1, ZW, 14)
                zab = zab_v.bitcast(BF16).rearrange("p (j n) -> p j n", j=2)
                wp_v, Rwp = AR.carve("wp", b1 + ZW, 2 * 16 * 64, 2)
                wpb = wp_v.bitcast(BF16).rearrange("p (j i n) -> p j i n", j=2, i=16)
                sm_v, Rsm = AR.carve("poolsm", b1 + ZW + 2048, 136, 4)
                pstg_v, Rpstg = AR.carve("pstg", b1 + ZW + 2048 + 136, 256, 1)
                fst_v, Rfst = AR.carve("fst", b1 + ZW + 2048 + 136 + 256, 2816, 1)
                sl = wslot()
                wv = WS[sl][:, 0:2048].rearrange("p (k n) -> p k n", k=8)
                wdma(sl, wv, Win[:, :, 0:256])
                for q in range(2):
                    S.dma("sp", STG[q][0:120, 0:256], sp_[l, 8 * q:8 * q + 8].rearrange("b i c -> (b i) c"), DSTG[q], writes=[RSTG[q]])
                S.dma("sp", fst_v[0:32, 0:2816], sf_[l].rearrange("b i c -> (b i) c"), DM["fst"], writes=[Rfst[0]])
                out_dma(o_pool_s[l][:, 0:11, :], sp_[l][:, 4:15, :], [], "d2d")
                WMAX = [4, 16]
                for j in range(2):
                    for i in range(WMAX[j]):
                        S.op("pool", lambda e: e.tensor_scalar(wpb[:, j, i, :], PWf[:, j, :], CST[:, C_CT + 16 * j + i:C_CT + 16 * j + i + 1], 0.0, ALU.mult, ALU.add),
                             reads=[Rpwf, Rcst], writes=[Rwp[j]])
                zsA = [sm_v[:, 0:31], sm_v[:, 31:62]]
                RzsA = [Rsm[0], Rsm[1]]
                for j in range(2):
                    RZ = Rzab[7 * j:7 * j + 7]
                    zs = zsA[j]
                    S.op("dve", lambda e: e.memset(zab[:, j, 0:15], 0.0), writes=[RZ[0]])
                    S.op("dve", lambda e: e.memset(zs[:, 0:15], 0.0), writes=[RzsA[j]])
                    for ti, (c0, n) in enumerate(TTS):
                        pb, Rpb = bank()
                        for k in range(KC):
                            S.op("pe", lambda e: e.matmul(pb[:, 0:n], wv[:, k, j * 128:(j + 1) * 128], H[:, k, c0:c0 + n], start=(k == 0), stop=(k == 7)),
                                 reads=[RWS[sl], RH[ti]], writes=[Rpb], inc=(k == 7))
                        dst = zab[:, j, 15 + c0:15 + c0 + n] if ti < 4 else zab[:, j, PPW + 240:PPW + 304]
                        S.op("act", lambda e: e.copy(dst, pb[:, 0:n]), reads=[Rpb], writes=[RZ[1 + ti]])
                        if ti == 0:
                            S.op("dve", lambda e: e.tensor_copy(zs[:, 15:31], pb[:, 0:16]), reads=[Rpb], writes=[RzsA[j]])
                for j in range(2):
                    RZ = Rzab[7 * j:7 * j + 7]
                    zs = zsA[j]
                    ssa = sm_v[:, 62:93]
                    ssb = sm_v[:, 93:124]
                    pf = sm_v[:, 124:132].bitcast(BF16)
                    for q in range(2):
                        s_ = q % 2
                        pb, Rpb = bank()
                        S.op("pe", lambda e: e.transpose(pb[:, 0:120], STG[s_][0:120, j * 128:(j + 1) * 128], ident[0:120, 0:120]),
                             reads=[RSTG[s_], Rcst], writes=[Rpb])
                        dst = zab[:, j, PPW:PPW + 240].rearrange("p (i b) -> p i b", b=16)[:, :, 8 * q:8 * q + 8]
                        S.op("act", lambda e: e.copy(dst, pb[:, 0:120].rearrange("p (b i) -> p i b", i=15)), reads=[Rpb], writes=[RZ[6]])
                    for ti, (c0, n) in enumerate(TTS):
                        pb, Rpb = bank()
                        for i in range(WMAX[j]):
                            if ti < 4:
                                src = zab[:, j, 15 + c0 - i:15 + c0 - i + n]
                            else:
                                src = zab[:, j, PPW + 240 - 16 * i:PPW + 304 - 16 * i]
                            S.op("pe", lambda e: e.matmul(pb[:, 0:n], wpb[:, j, i, :], src, start=(i == 0), stop=(i == WMAX[j] - 1)),
                                 reads=([RZ[ti], RZ[1 + ti]] if ti < 4 else [RZ[5], RZ[6]]) + [Rwp[j]], writes=[Rpb], inc=(i == WMAX[j] - 1))
                        S.op("act", lambda e: e.activation(aout[:, j, c0:c0 + n], pb[:, 0:n], AF.Identity,
                                                           scale=VT[:, VT_PS + l * 2 + j:VT_PS + l * 2 + j + 1]),
                             reads=[Rpb, Rvt], writes=[Rao[ti]])
                    def wsum_s(dst_v, src_v, sh, p0, p1, rd, wr):
                        S.op("dve", lambda e: e.tensor_tensor(dst_v[p0:p1, sh:31], src_v[p0:p1, sh:31], src_v[p0:p1, 0:31 - sh], ALU.add),
                             reads=rd, writes=wr)
                    R0, R1, R2, R3 = [RzsA[j]], [Rsm[2]], [Rsm[3]], [Rsm[3]]
                    if j == 0:
                        wsum_s(ssa, zs, 1, 64, 128, R0, R1)
                        wsum_s(ssb, zs, 1, 0, 64, R0, R2)
                        wsum_s(ssb, ssa, 2, 64, 128, R1, R2)
                    else:
                        wsum_s(ssa, zs, 1, 0, 128, R0, R1)
                        wsum_s(ssb, ssa, 2, 0, 128, R1, R2)
                        wsum_s(ssa, ssb, 4, 0, 128, R2, R1)
                        wsum_s(ssb, ssa, 8, 64, 128, R1, R2)
                        S.op("dve", lambda e: e.tensor_copy(ssb[0:64, 15:31], ssa[0:64, 15:31]), reads=R1, writes=R2)
                    ic = CST[:, C_INVC + 16 * j:C_INVC + 16 * j + 16]
                    S.op("dve", lambda e: e.tensor_tensor(ssb[:, 15:31], ssb[:, 15:31], ic, ALU.mult), reads=[Rcst], writes=R2)
                    S.op("dve", lambda e: e.tensor_tensor(pf[:, 0:16], ssb[:, 15:31], zs[:, 15:31], ALU.subtract), reads=R2 + R0, writes=R3)
                    pb, Rpb = bank()
                    S.op("pe", lambda e: e.matmul(pb[:, 0:16], PW[:, j, :], pf[:, 0:16], start=True, stop=True), reads=R3 + [Rpw], writes=[Rpb])
                    S.op("act", lambda e: e.activation(aout[:, j, 0:16], pb[:, 0:16], AF.Identity,
                                                       scale=VT[:, VT_PS + l * 2 + j:VT_PS + l * 2 + j + 1]),
                         reads=[Rpb, Rvt], writes=[Rao[0]])
                pt, Rpt = bank()
                for k in range(KC):
                    S.op("pe", lambda e: e.matmul(pt[0:79, 0:256], H[:, k, NT - 79:NT], wv[:, k, 0:256], start=(k == 0), stop=(k == 7)),
                         reads=[RWS[sl], RH[3], RH[4]], writes=[Rpt], inc=(k == 7))
                S.op("act", lambda e: e.copy(pstg_v[0:79, 0:256], pt[0:79, 0:256]), reads=[Rpt], writes=[Rpstg[0]])
                out_dma(o_pool_p[l], pstg_v[0:15, 0:256], [Rpstg[0]], "pstg")
                for t in range(4):
                    out_dma(o_pool_s[l][:, 11 + t, :], pstg_v[15 + 16 * t:31 + 16 * t, 0:256], [Rpstg[0]], "pstg")

                for f in range(NFF):
                    pb, Rpb = bank()
                    S.op("pe", lambda e: e.transpose(pb[:, 0:32], fst_v[0:32, f * 128:(f + 1) * 128], ident[0:32, 0:32]),
                         reads=[Rfst[0], Rcst], writes=[Rpb])
                    S.op("act", lambda e: e.copy(FS[:, f, :].rearrange("p (i b) -> p i b", b=16), pb[:, 0:32].rearrange("p (b i) -> p i b", i=2)),
                         reads=[Rpb], writes=[Rfs])
                mark(f'ga{l}')
                bout_v, Rbo = AR.carve("bout", BB_, 2 * NT, 5)
                bout = bout_v.bitcast(BF16).rearrange("p (h n) -> p h n", h=4)
                c1_ = 0
                vb_v, Rvb = AR.carve("vb", c1_, 17 * 192, 17)
                vb = vb_v.bitcast(BF16).rearrange("p (c n) -> p c n", c=17)
                c2_ = c1_ + 17 * 192
                tt_v, Rtt = AR.carve("ttmp", c2_, 2 * 512, 2)
                vs_v, Rvs = AR.carve("vstg", c2_ + 1024, 384, 1)
                slv = wslot()
                wvv = WS[slv][:, 0:3072].rearrange("p (k n) -> p k n", k=8)
                wdma(slv, wvv, Win[:, :, 640:1024])
                slu = wslot()
                wvu = WS[slu][:, 0:3072].rearrange("p (k n) -> p k n", k=8)
                wdma(slu, wvu, Win[:, :, 256:640])
                mark(f'gbva{l}')
                for c in range(17):
                    if c == 1:
                        mark(f'gbvb{l}')
                    if c == 16:
                        mark(f'gbvc{l}')
                    rows = 128 if c < 16 else 64
                    ti = c // 4
                    pb, Rpb = bank()
                    for k in range(KC):
                        S.op("pe", lambda e: e.matmul(pb[0:rows, 0:384], H[:, k, c * 128:c * 128 + rows], wvv[:, k, :], start=(k == 0), stop=(k == 7)),
                             reads=[RWS[slv], RH[ti]], writes=[Rpb], inc=(k == 7))
                    S.op("act", lambda e: e.copy(vb[0:rows, c, :], pb[0:rows, 0:384]), reads=[Rpb], writes=[Rvb[c]])
                    if c == 16:
                        S.op("dve", lambda e: e.tensor_copy(vs_v[0:64, 0:384], pb[0:64, 0:384]), reads=[Rpb], writes=[Rvs[0]])
                        for t in range(4):
                            out_dma(o_v_s[l][:, t, :], vs_v[16 * t:16 * t + 16, 0:384], [Rvs[0]], "vstg")
                mark(f'gbv{l}')
                for h in range(4):
                    for ti, (c0, n) in enumerate(TTS):
                        if ti == 4:
                            mark(f'gbh{l}')
                        pS, RpS = bank()
                        b = ti % 2
                        T_ = tt_v[0:96, b * 512:b * 512 + n]
                        if ti < 4:
                            for cc in range(4):
                                c = ti * 4 + cc
                                S.op("pe", lambda e: e.matmul(pS[0:96, cc * 128:(cc + 1) * 128], vb[:, c, h * 96:(h + 1) * 96], WT[:, h, :],
                                                              start=True, stop=True, skip_group_check=True),
                                     reads=[Rvb[c], Rwt], writes=[RpS], inc=(cc == 3))
                            S.op("dve", lambda e: e.tensor_tensor(T_.rearrange("p (c n) -> p c n", c=4), pS[0:96, 0:512].rearrange("p (c n) -> p c n", c=4),
                                                                  BB[:, h, 0:128].unsqueeze(1).broadcast_to([96, 4, 128]), ALU.add),
                                 reads=[RpS, Rbb], writes=[Rtt[b]])
                        else:
                            S.op("pe", lambda e: e.matmul(pS[0:96, 0:64], vb[0:64, 16, h * 96:(h + 1) * 96], WTS[:, h, :],
                                                          start=True, stop=True),
                                 reads=[Rvb[16], Rwts], writes=[RpS])
                            S.op("dve", lambda e: e.tensor_tensor(T_, pS[0:96, 0:64], BB[:, h, 128:192], ALU.add),
                                 reads=[RpS, Rbb], writes=[Rtt[b]])
                        pU, RpU = bank()
                        for k in range(KC):
                            S.op("pe", lambda e: e.matmul(pU[0:96, 0:n], wvu[:, k, h * 96:(h + 1) * 96], H[:, k, c0:c0 + n], start=(k == 0), stop=(k == 7)),
                                 reads=[RWS[slu], RH[ti]], writes=[RpU], inc=(k == 7))
                        S.op("dve", lambda e: e.tensor_tensor(bout[0:96, h, c0:c0 + n], pU[0:96, 0:n], T_, ALU.mult),
                             reads=[RpU, Rtt[b]], writes=[Rbo[ti]])

                mark(f'gb{l}')
                for mm in range(4):
                    sl = wslot()
                    wo = WS[sl][:, 0:2304].rearrange("p (k n) -> p k n", k=9)
                    cs = slice(mm * 256, mm * 256 + 256)
                    wdma(sl, wo[:, 0:2, :], Wout[0:256, cs].rearrange("(k p) n -> p k n", p=128))
                    wdma(sl, wo[0:96, 2:6, :], Wout[256:640, cs].rearrange("(k p) n -> p k n", p=96))
                    wdma(sl, wo[:, 6:9, :], Wout[640:1024, cs].rearrange("(k p) n -> p k n", p=128))
                    def oproj_block(m2, ti):
                        m = mm * 2 + m2
                        ms = slice(m2 * 128, m2 * 128 + 128)
                        c0, n = TTS[ti]
                        pb, Rpb = bank()
                        ops = []
                        for j in range(2):
                            ops.append((wo[:, j, ms], aout[:, j, c0:c0 + n], Rao[ti]))
                        for h in range(4):
                            ops.append((wo[0:96, 2 + h, ms], bout[0:96, h, c0:c0 + n], Rbo[ti]))
                        for j in range(3):
                            ops.append((wo[:, 6 + j, ms], cout[:, j, c0:c0 + n], Rco[ti]))
                        for q, (lh, rh, rr) in enumerate(ops):
                            S.op("pe", lambda e: e.matmul(pb[:, 0:n], lh, rh, start=(q == 0), stop=(q == 8)),
                                 reads=[RWS[sl], rr], writes=[Rpb], inc=(q == 8))
                        S.op("dve", lambda e: e.tensor_tensor(X[:, m, c0:c0 + n], pb[:, 0:n], X[:, m, c0:c0 + n], ALU.add),
                             reads=[Rpb], writes=[RX[m][ti]])

                    if mm < 3:
                        for m2 in range(2):
                            for ti in range(5):
                                oproj_block(m2, ti)
                    else:
                        nctx2 = norm_begin()
                        o_order = [4, 0, 1, 2, 3]
                        for pos, ti in enumerate(o_order):
                            for m2 in range(2):
                                oproj_block(m2, ti)
                            if pos >= 1:
                                norm_tt(nctx2, VT_N2 + l * 8, o_order[pos - 1])
                        norm_tt(nctx2, VT_N2 + l * 8, o_order[4])

                mark(f'op{l}')
                if l + 1 < DEPTH:
                    prep_dma(l + 1)
                mark(f'n2_{l}')
                GPW = 2 + NP_
                GSW = 6 * 16
                GW = GPW + GSW
                nb0 = 768 + 2048
                g_v, Rg = AR.carve("graw", nb0, 2 * GW, 14)
                tm_v, Rtm = AR.carve("ftmp", nb0 + 2 * GW, 3 * 512, 3)
                fo_v, Rfo = AR.carve("fostg", nb0 + 2 * GW + 1536, 2 * 128, 2)
                ab0 = nb0 + 2 * GW + 1536 + 256
                actb_v, Rab = AR.carve("actb", ab0, 8 * NT // 2, 5)
                actb = actb_v.bitcast(BF16).rearrange("p (f n) -> p f n", f=8)
                tmi = 0
                pend = {"f": None}
                for (f0, nf) in FF_GROUPS:
                    for fl in range(nf):
                        f = f0 + fl
                        sl = wslot()
                        wv = WS[sl][:, 0:2048].rearrange("p (k n) -> p k n", k=8)
                        wdma(sl, wv[:, :, 0:128], Wup[:, :, f * 128:(f + 1) * 128])
                        wdma(sl, wv[:, :, 128:256], Wup[:, :, D_FF + f * 128:D_FF + (f + 1) * 128])
                        gb = f % 2
                        G = g_v[:, gb * GW:(gb + 1) * GW]
                        S.op("dve", lambda e: e.memset(G[:, 0:2], 0.0), writes=[Rg[7 * gb]])
                        S.op("dve", lambda e: e.tensor_copy(G[:, GPW:GPW + 32], FS[:, f, :]), reads=[Rfs], writes=[Rg[7 * gb + 5]])
                        w0 = VT[:, VT_FW + (l * 3 + 0) * NFF + f:VT_FW + (l * 3 + 0) * NFF + f + 1]
                        w1 = VT[:, VT_FW + (l * 3 + 1) * NFF + f:VT_FW + (l * 3 + 1) * NFF + f + 1]
                        w2 = VT[:, VT_FW + (l * 3 + 2) * NFF + f:VT_FW + (l * 3 + 2) * NFF + f + 1]
                        fb = VT[:, VT_FB + l * NFF + f:VT_FB + l * NFF + f + 1]
                        for ti, (c0, n) in enumerate(TTS):
                            pg, Rpg = bank()
                            for k in range(KC):
                                S.op("pe", lambda e: e.matmul(pg[:, 0:n], wv[:, k, 0:128], H[:, k, c0:c0 + n], start=(k == 0), stop=(k == 7)),
                                     reads=[RWS[sl], RH[ti]], writes=[Rpg], inc=(k == 7))
                            pv, Rpv = bank()
                            for k in range(KC):
                                S.op("pe", lambda e: e.matmul(pv[:, 0:n], wv[:, k, 128:256], H[:, k, c0:c0 + n], start=(k == 0), stop=(k == 7)),
                                     reads=[RWS[sl], RH[ti]], writes=[Rpv], inc=(k == 7))
                            if ti < 4:
                                gc, sh = 2 + c0, 1
                            else:
                                gc, sh = GPW + 32, 16
                            gw_ = Rg[7 * gb + 1 + ti] if ti < 4 else Rg[7 * gb + 6]
                            gr_ = [Rg[7 * gb + ti], Rg[7 * gb + 1 + ti]] if ti < 4 else [Rg[7 * gb + 5], Rg[7 * gb + 6]]
                            S.op("act", lambda e: e.copy(G[:, gc:gc + n], pg[:, 0:n]), reads=[Rpg], writes=[gw_])
                            q = tmi % 3
                            tmi += 1
                            T_ = tm_v[:, q * 512:q * 512 + n]
                            S.op("act", lambda e: e.activation(T_, pg[:, 0:n], AF.Identity, bias=fb, scale=w2), reads=[Rpg, Rvt], writes=[Rtm[q]])
                            S.op("dve", lambda e: e.scalar_tensor_tensor(T_, G[:, gc - sh:gc - sh + n], w1, T_, ALU.mult, ALU.add),
                                 reads=gr_ + [Rvt], writes=[Rtm[q]])
                            S.op("dve", lambda e: e.scalar_tensor_tensor(T_, G[:, gc - 2 * sh:gc - 2 * sh + n], w0, T_, ALU.mult, ALU.add),
                                 reads=gr_ + [Rvt], writes=[Rtm[q]])
                            def stage2(T_=T_, q=q, pv=pv, Rpv=Rpv, fl=fl, c0=c0, n=n, ti=ti):
                                S.op("act", lambda e: e.activation(T_, T_, AF.Silu), reads=[Rtm[q]], writes=[Rtm[q]])
                                S.op("dve", lambda e: e.tensor_tensor(actb[:, fl, c0:c0 + n], pv[:, 0:n], T_, ALU.mult),
                                     reads=[Rpv, Rtm[q]], writes=[Rab[ti]])
                            if pend["f"] is not None:
                                pend["f"]()
                            pend["f"] = stage2
                        pt, Rpt = bank()
                        for k in range(KC):
                            S.op("pe", lambda e: e.matmul(pt[0:66, 0:128], H[:, k, NT - 66:NT], wv[:, k, 0:128], start=(k == 0), stop=(k == 7)),
                                 reads=[RWS[sl], RH[3], RH[4]], writes=[Rpt], inc=(k == 7))
                        ob = f % 2
                        S.op("act", lambda e: e.copy(fo_v[0:66, ob * 128:(ob + 1) * 128], pt[0:66, 0:128]), reads=[Rpt], writes=[Rfo[ob]])
                        out_dma(o_ffn_p[l][:, f * 128:(f + 1) * 128], fo_v[0:2, ob * 128:(ob + 1) * 128], [Rfo[ob]], f"fo{ob}")
                        for i in range(2):
                            out_dma(o_ffn_s[l][:, i, f * 128:(f + 1) * 128], fo_v[2 + 16 * (2 + i):18 + 16 * (2 + i), ob * 128:(ob + 1) * 128],
                                    [Rfo[ob]], f"fo{ob}")
                    if pend["f"] is not None:
                        pend["f"]()
                        pend["f"] = None
                    last = (f0 + nf == NFF)
                    if not last:
                        for mm in range(4):
                            sl = wslot()
                            wd = WS[sl][:, 0:nf * 256].rearrange("p (k n) -> p k n", k=nf)
                            wdma(sl, wd, Wdn[:, f0:f0 + nf, mm * 256:(mm + 1) * 256])
                            for m2 in range(2):
                                m = mm * 2 + m2
                                for ti, (c0, n) in enumerate(TTS):
                                    pb, Rpb = bank()
                                    for kk in range(nf):
                                        S.op("pe", lambda e: e.matmul(pb[:, 0:n], wd[:, kk, m2 * 128:(m2 + 1) * 128], actb[:, kk, c0:c0 + n],
                                                                      start=(kk == 0), stop=(kk == nf - 1)),
                                             reads=[RWS[sl], Rab[ti]], writes=[Rpb], inc=(kk == nf - 1))
                                    S.op("dve", lambda e: e.tensor_tensor(X[:, m, c0:c0 + n], pb[:, 0:n], X[:, m, c0:c0 + n], ALU.add),
                                         reads=[Rpb], writes=[RX[m][ti]])
                    else:
                        wparts = []
                        for k0 in range(0, nf, 3):
                            nk = min(3, nf - k0)
                            sl = wslot()
                            wvw = WS[sl][:, 0:nk * 1024].rearrange("p (k n) -> p k n", k=nk)
                            wdma(sl, wvw, Wdn[:, f0 + k0:f0 + k0 + nk, :])
                            wparts.append((sl, wvw, k0))
                        if l + 1 < DEPTH:
                            prep(l + 1)
                        nctx = norm_begin()
                        is_fin = (l + 1 == DEPTH)
                        if is_fin:
                            fcb, Tq = make_final_cb()

                        tails = {}

                        def head(ti):
                            if not is_fin:
                                tails[ti] = norm_tt(nctx, VT_N1 + (l + 1) * 8, ti, defer=True)
                            else:
                                tails[ti] = norm_tt(nctx, VT_FN, ti, final=True, ycb=fcb, defer=True)

                        order = [4, 0, 1, 2, 3]
                        for pos, ti in enumerate(order):
                            c0, n = TTS[ti]
                            for m in range(8):
                                pb, Rpb = bank()
                                for kk in range(nf):
                                    sl, wvw, k0 = wparts[kk // 3]
                                    S.op("pe", lambda e: e.matmul(pb[:, 0:n], wvw[:, kk - k0, m * 128:(m + 1) * 128], actb[:, kk, c0:c0 + n],
                                                                  start=(kk == 0), stop=(kk == nf - 1)),
                                         reads=[RWS[sl], Rab[ti]], writes=[Rpb], inc=(kk == nf - 1))
                                S.op("dve", lambda e: e.tensor_tensor(X[:, m, c0:c0 + n], pb[:, 0:n], X[:, m, c0:c0 + n], ALU.add),
                                     reads=[Rpb], writes=[RX[m][ti]])
                            if pos >= 1:
                                head(order[pos - 1])
                            if is_fin and pos >= 3:
                                Tq[order[pos - 3]]()
                            if pos >= 2:
                                tails[order[pos - 2]]()
                        head(order[4])
                        if is_fin:
                            Tq[order[2]]()
                        tails[order[3]]()
                        if is_fin:
                            Tq[order[3]]()
                        tails[order[4]]()
                        if is_fin:
                            Tq[order[4]]()

            mark('layers')

        except _Stop:
            pass

        sp = S.E["sp"]["eng"]
        for d in DOUT.values():
            if d.cnt:
                sp.wait_ge(d.sem, d.cnt)
        for d in S.dsems:
            assert d.cnt < 60000, d.cnt
        for e in S.E.values():
            assert e["cnt"] < 60000
    return nc


def _consts():
    c = np.zeros((128, C_N), np.float32)
    c[:, C_ID:C_ID + 128] = np.eye(128, dtype=np.float32)
    s = np.arange(128)
    c[:, C_TRI:C_TRI + 128] = (s[:, None] <= s[None, :]).astype(np.float32)
    idx = np.arange(64)
    t_, b_ = idx // 16, idx % 16
    c[0:64, C_M64:C_M64 + 64] = ((b_[:, None] == b_[None, :]) & (t_[:, None] <= t_[None, :])).astype(np.float32)
    c[0:4, C_E4:C_E4 + 64] = (np.arange(4)[:, None] == t_[None, :]).astype(np.float32)
    wins = np.array([[2, 4], [8, 16]], np.float32)
    for j in range(2):
        w = np.where(np.arange(128) < 64, wins[j, 0], wins[j, 1]).astype(np.float32)
        c[:, C_INVW + j] = 1.0 / w
        for t in range(16):
            c[:, C_INVC + 16 * j + t] = 1.0 / np.minimum(t + 1.0, w)
            c[:, C_CT + 16 * j + t] = np.where(t < w, 1.0 / w, 0.0) - (1.0 if t == 0 else 0.0)
    return c


_NC_CACHE = {}


def kernel(x_prompt, x_sample, state_pool, state_conv, state_ffn_conv,
           norm1, w_in, pool_w, pool_scale, sgu_w, sgu_b, conv_w, conv_b,
           cnorm_g, cnorm_b, w_out, norm2, w_up, ffn_conv_w, ffn_conv_b,
           w_down, final_norm):
    f = lambda a: np.ascontiguousarray(np.asarray(a, dtype=np.float32))
    if "nc" not in _NC_CACHE:
        _NC_CACHE["nc"] = build_nc()
    nc = _NC_CACHE["nc"]
    shared = dict(norm1=f(norm1), w_in=f(w_in), pool_w=f(pool_w), pool_scale=f(pool_scale), sgu_w=f(sgu_w),
                  sgu_b=f(sgu_b), conv_w=f(conv_w), conv_b=f(conv_b), cnorm_g=f(cnorm_g), cnorm_b=f(cnorm_b),
                  w_out=f(w_out), norm2=f(norm2), w_up=f(w_up), ffn_conv_w=f(ffn_conv_w), ffn_conv_b=f(ffn_conv_b),
                  w_down=f(w_down), final_norm=f(final_norm), cst=_consts())
    x_prompt, x_sample = f(x_prompt), f(x_sample)
    state_pool, state_conv, state_ffn_conv = f(state_pool), f(state_conv), f(state_ffn_conv)
    in_maps = []
    for c in range(8):
        m = dict(shared)
        m["xp"] = x_prompt[c]
        m["xs"] = x_sample[16 * c:16 * c + 16]
        m["st_pool"] = np.ascontiguousarray(state_pool[:, 16 * c:16 * c + 16])
        m["st_conv"] = np.ascontiguousarray(state_conv[:, 16 * c:16 * c + 16])
        m["st_ffn"] = np.ascontiguousarray(state_ffn_conv[:, 16 * c:16 * c + 16])
        in_maps.append(m)
    res = run_bass_kernel_spmd(nc, in_maps, core_ids=list(range(8)))
    R = res.results
    cat0 = lambda k: np.stack([R[c][k] for c in range(8)], axis=0)
    y_p = cat0("y_p")
    y_s = np.concatenate([R[c]["y_s"] for c in range(8)], axis=0)
    pool_p = np.stack([R[c]["o_pool_p"] for c in range(8)], axis=1)
    pool_s = np.concatenate([R[c]["o_pool_s"] for c in range(8)], axis=1)
    conv_p = np.stack([R[c]["o_conv_p"] for c in range(8)], axis=1)
    conv_s = np.concatenate([R[c]["o_conv_s"] for c in range(8)], axis=1)
    ffn_p = np.stack([R[c]["o_ffn_p"] for c in range(8)], axis=1)
    ffn_s = np.concatenate([R[c]["o_ffn_s"] for c in range(8)], axis=1)
    v_s = np.concatenate([R[c]["o_v_s"] for c in range(8)], axis=1)
    return (y_p, y_s, pool_p, pool_s, conv_p, conv_s, ffn_p, ffn_s, v_s)
```

```python
import numpy as np
from contextlib import ExitStack
import concourse.bass as bass
import concourse.mybir as mybir
from concourse.bass_utils import run_bass_kernel_spmd

F32 = mybir.dt.float32
BF16 = mybir.dt.bfloat16
AF = mybir.ActivationFunctionType
ALU = mybir.AluOpType

DEPTH = 4
D = 1024
KC = 8
NP_ = 2048
NS = 64
NT = NP_ + NS
TTS = [(0, 512), (512, 512), (1024, 512), (1536, 512), (2048, 64)]
D_A, D_B, D_C = 256, 384, 384
D_FF = 2816
NFF = 22
D_IN = 1792
EPS = 1e-6
FF_GROUPS = [(0, 8), (8, 8), (16, 6)]

VT_N1 = 0
VT_N2 = VT_N1 + 32
VT_FN = VT_N2 + 32
VT_PS = VT_FN + 8
VT_CW = VT_PS + 8
VT_CB = VT_CW + 372
VT_CG = VT_CB + 12
VT_CBB = VT_CG + 12
VT_FW = VT_CBB + 12
VT_FB = VT_FW + 264
VT_N = VT_FB + 88

C_ID = 0
C_TRI = 128
C_M64 = 256
C_E4 = 320
C_INVW = 384
C_INVC = 386
C_CT = 418
C_N = 450


class Res:
    __slots__ = ("name", "w", "r", "excl")

    def __init__(self, name, inherit=None, excl=False):
        self.name = name
        self.w = None
        self.r = dict(inherit) if inherit else {}
        self.excl = excl


class DSem:
    __slots__ = ("sem", "cnt")

    def __init__(self, sem):
        self.sem = sem
        self.cnt = 0


class Sched:
    def __init__(self, nc, stack):
        self.nc = nc
        self.stack = stack
        self.E = {}
        for name, eng in [("pe", nc.tensor), ("act", nc.scalar), ("dve", nc.vector),
                          ("pool", nc.gpsimd), ("sp", nc.sync)]:
            sem = stack.enter_context(nc.semaphore("s_" + name))
            self.E[name] = {"eng": eng, "sem": sem, "cnt": 0, "waited": {}}
        self.dsems = []

    def dsem(self, name):
        d = DSem(self.stack.enter_context(self.nc.semaphore(name)))
        self.dsems.append(d)
        return d

    def _deps(self, reads, writes):
        deps = {}
        for r in reads:
            if r.w is not None and deps.get(r.w[0], 0) < r.w[1]:
                deps[r.w[0]] = r.w[1]
            if r.excl:
                for s, v in r.r.items():
                    if deps.get(s, 0) < v:
                        deps[s] = v
        for w in writes:
            if w.w is not None and deps.get(w.w[0], 0) < w.w[1]:
                deps[w.w[0]] = w.w[1]
            for s, v in w.r.items():
                if deps.get(s, 0) < v:
                    deps[s] = v
        return deps

    def _emit_waits(self, ename, deps):
        st = self.E[ename]
        for s, v in deps.items():
            if s == st["sem"] and (v > st["cnt"] or ename == "pe"):
                continue
            if st["waited"].get(s, 0) < v:
                for oe in self.E.values():
                    if oe["sem"] == s:
                        assert v <= oe["cnt"], ("wait on deferred event", ename, v, oe["cnt"])
                st["eng"].wait_ge(s, v)
                st["waited"][s] = v

    def _record(self, ev, reads, writes):
        s, v = ev
        for r in reads:
            if r.r.get(s, 0) < v:
                r.r[s] = v
        for w in writes:
            w.w = ev
            w.r = {}

    def op(self, ename, fn, reads=(), writes=(), inc=True):
        st = self.E[ename]
        self._emit_waits(ename, self._deps(reads, writes))
        ins = fn(st["eng"])
        ev = (st["sem"], st["cnt"] + 1)
        if inc:
            ins.then_inc(st["sem"], 1)
            st["cnt"] += 1
        self._record(ev, reads, writes)
        return ins

    def dma(self, qname, out, in_, ds, reads=(), writes=(), **kw):
        st = self.E[qname]
        deps = self._deps(reads, writes)
        deps.pop(ds.sem, None)
        self._emit_waits(qname, deps)
        ins = st["eng"].dma_start(out=out, in_=in_, **kw)
        ds.cnt += 16
        ins.then_inc(ds.sem, 16)
        self._record((ds.sem, ds.cnt), reads, writes)
        return ins

    def wait_all(self, ename, ress):
        deps = {}
        for r in ress:
            if r.w is not None and deps.get(r.w[0], 0) < r.w[1]:
                deps[r.w[0]] = r.w[1]
            for s, v in r.r.items():
                if deps.get(s, 0) < v:
                    deps[s] = v
        self._emit_waits(ename, deps)


class Arena:
    def __init__(self, ap, nwords):
        self.ap = ap
        self.n = nwords
        self.live = []

    def carve(self, name, start, nwords, nres=1):
        end = start + nwords
        assert 0 <= start and end <= self.n, (name, start, end, self.n)
        inh = {}
        keep = []
        for (a, b, rs) in self.live:
            if a < end and start < b:
                for r in rs:
                    if r.w is not None and inh.get(r.w[0], 0) < r.w[1]:
                        inh[r.w[0]] = r.w[1]
                    for s, v in r.r.items():
                        if inh.get(s, 0) < v:
                            inh[s] = v
                if not (start <= a and b <= end):
                    keep.append((a, b, rs))
            else:
                keep.append((a, b, rs))
        rs = [Res(f"{name}{i}", inh) for i in range(nres)]
        keep.append((start, end, rs))
        self.live = keep
        return self.ap[:, start:end], rs


class _Stop(Exception):
    pass


def build_nc(stop=None):
    nc = bass.Bass("TRN2", target_bir_lowering=False)

    def mark(k):
        if stop is not None and k == stop:
            raise _Stop()

    def din(name, shape):
        return nc.dram_tensor(name, list(shape), F32, kind="ExternalInput").ap()

    def dout(name, shape):
        return nc.dram_tensor(name, list(shape), F32, kind="ExternalOutput").ap()

    xp = din("xp", [NP_, D])
    xs = din("xs", [16, 4, D])
    sp_ = din("st_pool", [DEPTH, 16, 15, D_A])
    sc_ = din("st_conv", [DEPTH, 16, 30, D_C])
    sf_ = din("st_ffn", [DEPTH, 16, 2, D_FF])
    norm1 = din("norm1", [DEPTH, D])
    w_in = din("w_in", [DEPTH, D, D_IN])
    pool_w = din("pool_w", [DEPTH, 4, 64, 64])
    pool_scale = din("pool_scale", [DEPTH, D_A])
    sgu_w = din("sgu_w", [DEPTH, 4, 128, 128])
    sgu_b = din("sgu_b", [DEPTH, 4, 128])
    conv_w = din("conv_w", [DEPTH, 31, D_C])
    conv_b = din("conv_b", [DEPTH, D_C])
    cnorm_g = din("cnorm_g", [DEPTH, D_C])
    cnorm_b = din("cnorm_b", [DEPTH, D_C])
    w_out = din("w_out", [DEPTH, D, D])
    norm2 = din("norm2", [DEPTH, D])
    w_up = din("w_up", [DEPTH, D, 2 * D_FF])
    ffn_conv_w = din("ffn_conv_w", [DEPTH, 3, D_FF])
    ffn_conv_b = din("ffn_conv_b", [DEPTH, D_FF])
    w_down = din("w_down", [DEPTH, D_FF, D])
    final_norm = din("final_norm", [D])
    cst_d = din("cst", [128, C_N])

    y_p = dout("y_p", [NP_, D])
    y_s = dout("y_s", [16, 4, D])
    o_pool_p = dout("o_pool_p", [DEPTH, 15, D_A])
    o_pool_s = dout("o_pool_s", [DEPTH, 16, 15, D_A])
    o_conv_p = dout("o_conv_p", [DEPTH, 30, D_C])
    o_conv_s = dout("o_conv_s", [DEPTH, 16, 30, D_C])
    o_ffn_p = dout("o_ffn_p", [DEPTH, 2, D_FF])
    o_ffn_s = dout("o_ffn_s", [DEPTH, 16, 2, D_FF])
    o_v_s = dout("o_v_s", [DEPTH, 16, 4, D_B])

    with ExitStack() as st:
        S = Sched(nc, st)

        def sb(name, shape, dt):
            return st.enter_context(nc.sbuf_tensor(name, shape, dt))

        X = sb("X", [128, KC, NT], F32)
        H = sb("H", [128, KC, NT], BF16)
        NSLOT = 3
        WSA = sb("wsall", [128, NSLOT * 3072], BF16)
        WS = [WSA[:, i * 3072:(i + 1) * 3072] for i in range(NSLOT)]
        CST = sb("cstt", [128, C_N], F32)
        VT = sb("vt", [128, VT_N], F32)
        PW = sb("pw", [128, 2, 128], BF16)
        PWf = sb("pwf", [128, 2, 128], F32)
        ONES = sb("ones", [128, 128], BF16)
        IDB = sb("idb", [128, 128], BF16)
        EPST = sb("epst", [128, 1], F32)
        WT = sb("wt", [128, 4, 128], BF16)
        WTS = sb("wts", [64, 4, 64], BF16)
        W4 = sb("w4", [4, 4, 4], F32)
        BB = sb("bb", [96, 4, 192], F32)
        M1 = sb("m1", [4, 64], F32)
        STG = [sb(f"stg{i}", [128, 384], F32) for i in range(2)]
        FS = sb("fs", [128, NFF, 32], F32)
        WST = [sb(f"wst{i}", [128, 128], F32) for i in range(4)]
        AW = (nc.sbuf_bytes_remaining // 4) - 16
        ARENA_T = sb("arena", [128, AW], F32)
        AR = Arena(ARENA_T, AW)

        PS = [st.enter_context(nc.psum_tensor(f"ps{i}", [128, 512], F32)) for i in range(8)]
        RPS = [Res(f"ps{i}", excl=True) for i in range(8)]
        pstate = {"i": 0}

        reserved = set()

        def bank():
            while True:
                i = pstate["i"] % 8
                pstate["i"] += 1
                if i not in reserved:
                    pstate["last"] = i
                    return PS[i], RPS[i]

        ident = CST[:, C_ID:C_ID + 128]
        trimask = CST[:, C_TRI:C_TRI + 128]
        mask64 = CST[0:64, C_M64:C_M64 + 64]
        E4 = CST[0:4, C_E4:C_E4 + 64]

        RX = [[Res(f"x{k}_{t}") for t in range(5)] for k in range(KC)]
        RH = [Res(f"h{t}") for t in range(5)]
        RWS = [Res(f"ws{i}") for i in range(NSLOT)]
        DWS = [S.dsem(f"dws{i}") for i in range(NSLOT)]
        DWB = S.dsem("dwb")
        Rcst, Rvt, Rpw, Rpwf, Rones, Rwt, Rwts, Rbr, Rbrs, Rw4, Rm1, Rfs, Rbb = [
            Res(n) for n in "cst vt pw pwf ones wt wts br brs w4 m1 fs bb".split()]
        RSTG = [Res("stg0"), Res("stg1")]
        DSTG = [S.dsem("dstg0"), S.dsem("dstg1")]
        RWST = [Res(f"wst{i}") for i in range(4)]
        DWST = [S.dsem(f"dwst{i}") for i in range(4)]
        DM = {k: S.dsem("dm_" + k) for k in ["cst", "w4", "br", "brs", "pwf", "fst"]}
        ROUT = Res("out")
        DOUT = {k: S.dsem("do_" + k) for k in ["d2d", "cstg", "pstg", "vstg", "fo0", "fo1", "yt0", "yt1"]}

        def out_dma(dst, src, reads, key):
            S.dma("sp", dst, src, DOUT[key], reads=reads)

        wstate = {"i": 0}

        def wslot():
            i = wstate["i"] % NSLOT
            wstate["i"] += 1
            return i

        def wbig(dst, src):
            S.dma("pool", dst, src, DWB, writes=RWS)

        def wdma(slot, dst, src):
            S.dma("pool", dst, src, DWS[slot], writes=[RWS[slot]])

        try:
            S.dma("sp", CST[:], cst_d, DM["cst"], writes=[Rcst])
            S.op("dve", lambda e: e.memset(ONES[:], 1.0), writes=[Rones])
            S.op("dve", lambda e: e.memset(EPST[:], EPS), writes=[Rones])
            S.op("dve", lambda e: e.tensor_copy(IDB[:], CST[:, C_ID:C_ID + 128]), reads=[Rcst], writes=[Rones])

            vt_src = [
                (norm1.rearrange("l (k p) -> (l k) p", p=128), VT_N1),
                (norm2.rearrange("l (k p) -> (l k) p", p=128), VT_N2),
                (final_norm.rearrange("(k p) -> k p", p=128), VT_FN),
                (pool_scale.rearrange("l (k p) -> (l k) p", p=128), VT_PS),
                (conv_w.rearrange("l i (k p) -> (l i k) p", p=128), VT_CW),
                (conv_b.rearrange("l (k p) -> (l k) p", p=128), VT_CB),
                (cnorm_g.rearrange("l (k p) -> (l k) p", p=128), VT_CG),
                (cnorm_b.rearrange("l (k p) -> (l k) p", p=128), VT_CBB),
                (ffn_conv_w.rearrange("l i (k p) -> (l i k) p", p=128), VT_FW),
                (ffn_conv_b.rearrange("l (k p) -> (l k) p", p=128), VT_FB),
            ]
            bi = 0
            for src, off in vt_src:
                R = src.shape[0]
                for r0 in range(0, R, 128):
                    r = min(128, R - r0)
                    s_ = bi % 4
                    bi += 1
                    S.dma("sp", WST[s_][0:r, :], src[r0:r0 + r, :], DWST[s_], writes=[RWST[s_]])
                    pb, Rpb = bank()
                    S.op("pe", lambda e: e.transpose(pb[:, 0:r], WST[s_][0:r, :], ident[0:r, 0:r]),
                         reads=[RWST[s_], Rcst], writes=[Rpb])
                    S.op("act", lambda e: e.copy(VT[:, off + r0:off + r0 + r], pb[:, 0:r]), reads=[Rpb], writes=[Rvt])

            mark('xload')
            def norm_begin():
                sq_v, Rsq = AR.carve("sq", 0, 3 * 256, 3)
                rt_v, Rrt = AR.carve("rt", 768, 2 * 512, 2)
                ri_v, Rri = AR.carve("ri", 768 + 1024, 2 * 512, 2)
                return (sq_v.bitcast(BF16).rearrange("p (a n) -> p a n", a=3), Rsq, rt_v, Rrt, ri_v, Rri)

            def norm_tt(ctx, gcol, ti, final=False, ycb=None, defer=False):
                sqb, Rsq, rt_v, Rrt, ri_v, Rri = ctx
                c0, n = TTS[ti]
                pb, Rpb = bank()
                bidx = pstate["last"]
                if defer:
                    reserved.add(bidx)
                for kc in range(KC):
                    q = kc % 3
                    S.op("act", lambda e: e.activation(sqb[:, q, 0:n], X[:, kc, c0:c0 + n], AF.Square),
                         reads=[RX[kc][ti]], writes=[Rsq[q]])
                    S.op("pe", lambda e: e.matmul(pb[:, 0:n], ONES[:], sqb[:, q, 0:n], start=(kc == 0), stop=(kc == KC - 1)),
                         reads=[Rsq[q], Rones], writes=[Rpb])
                b = ti % 2
                rt = rt_v[:, b * 512:b * 512 + n]
                ri = ri_v[:, b * 512:b * 512 + n]

                def tail():
                    S.op("act", lambda e: e.activation(rt, pb[:, 0:n], AF.Ln, bias=EPST[:, 0:1], scale=1.0 / D),
                         reads=[Rpb, Rones], writes=[Rrt[b]])
                    S.op("act", lambda e: e.activation(ri, rt, AF.Exp, scale=-0.5), reads=[Rrt[b]], writes=[Rri[b]])
                    if not final:
                        for kc in range(KC):
                            S.op("dve", lambda e: e.scalar_tensor_tensor(H[:, kc, c0:c0 + n], X[:, kc, c0:c0 + n],
                                                                         VT[:, gcol + kc:gcol + kc + 1], ri, ALU.mult, ALU.mult),
                                 reads=[RX[kc][ti], Rri[b], Rvt], writes=[RH[ti]])
                    else:
                        ycb(ti, c0, n, ri, Rri[b])
                    reserved.discard(bidx)
                if defer:
                    return tail
                tail()

            def rmsnorm(gcol, final=False, ycb=None):
                ctx = norm_begin()
                for ti in range(5):
                    norm_tt(ctx, gcol, ti, final, ycb)

            def prep_dma(l):
                for h in range(4):
                    S.dma("sp", WST[h][:], sgu_w[l, h], DWST[h], writes=[RWST[h]])
                S.dma("sp", W4[:], sgu_w[l, :, 0:4, 0:4].rearrange("h a b -> a h b"), DM["w4"], writes=[Rw4])
                S.dma("sp", BB[:, :, 0:128], sgu_b[l].partition_broadcast(96), DM["br"], writes=[Rbb])

            def prep(l):
                for h in range(4):
                    s_ = h
                    pb, Rpb = bank()
                    S.op("pe", lambda e: e.transpose(pb[:, 0:128], WST[s_][:], ident), reads=[RWST[s_], Rcst], writes=[Rpb])
                    S.op("dve", lambda e: e.tensor_tensor(WT[:, h, :], pb[:, 0:128], trimask, ALU.mult),
                         reads=[Rpb, Rcst], writes=[Rwt])
                for h in range(4):
                    S.op("dve", lambda e: e.tensor_copy(BB[:, h, 128:192].rearrange("p (t b) -> p t b", b=16),
                                                        BB[:, h, 0:4].unsqueeze(2).broadcast_to([96, 4, 16])),
                         reads=[Rbb], writes=[Rbb])
                for h in range(4):
                    pb, Rpb = bank()
                    S.op("pe", lambda e: e.matmul(pb[0:4, 0:64], W4[:, h, :], E4, start=True, stop=True),
                         reads=[Rw4, Rcst], writes=[Rpb])
                    S.op("act", lambda e: e.copy(M1[:], pb[0:4, 0:64]), reads=[Rpb], writes=[Rm1])
                    pb2, Rpb2 = bank()
                    S.op("pe", lambda e: e.matmul(pb2[0:64, 0:64], E4, M1[:], start=True, stop=True),
                         reads=[Rm1, Rcst], writes=[Rpb2])
                    S.op("dve", lambda e: e.tensor_tensor(WTS[:, h, :], pb2[0:64, 0:64], mask64, ALU.mult),
                         reads=[Rpb2, Rcst], writes=[Rwts])
                S.op("dve", lambda e: e.memset(PWf[:], 0.0), writes=[Rpwf])
                for g in range(4):
                    j, q = g // 2, g % 2
                    S.dma("sp", PWf[64 * q:64 * q + 64, j, 64 * q:64 * q + 64], pool_w[l, g], DM["pwf"], writes=[Rpwf])
                S.op("dve", lambda e: e.tensor_copy(PW[:], PWf[:]), reads=[Rpwf], writes=[Rpw])


            def make_final_cb():
                yf_v, Ryf = AR.carve("yf", 2816, 4096, 1)
                yt0_v, Ryt0 = AR.carve("yt0", 2816 + 4096, 1024, 1)
                yt1_v, Ryt1 = AR.carve("yt1", 8900 + 6 * NT // 2, 1024, 1)
                yts = [yt0_v, yt1_v]
                Ryt = [Ryt0[0], Ryt1[0]]
                ycnt = {"i": 0}

                def final_cb(ti, c0, n, ri, Rri_):
                    yb = 0
                    Y = yf_v[:, 0:4096].rearrange("p (k n) -> p k n", k=8)
                    for kc in range(KC):
                        S.op("dve", lambda e: e.scalar_tensor_tensor(Y[:, kc, 0:n], X[:, kc, c0:c0 + n], VT[:, VT_FN + kc:VT_FN + kc + 1], ri,
                                                                     ALU.mult, ALU.mult),
                             reads=[RX[kc][ti], Rri_, Rvt], writes=[Ryf[yb]])
                    def T_stage(ti=ti, c0=c0, n=n):
                        nsub = 4 if ti < 4 else 1
                        rows = 128 if ti < 4 else 64
                        for a in range(nsub):
                            q = ycnt["i"] % 2
                            ycnt["i"] += 1
                            YT = yts[q]
                            for half in range(2):
                                pb, Rpb = bank()
                                for kq in range(4):
                                    kc = half * 4 + kq
                                    S.op("pe", lambda e: e.transpose(pb[0:rows, kq * 128:(kq + 1) * 128], Y[:, kc, a * 128:a * 128 + rows], ident),
                                         reads=[Ryf[yb], Rcst], writes=[Rpb], inc=(kq == 3))
                                if half == 0:
                                    S.op("act", lambda e: e.copy(YT[0:rows, 0:512], pb[0:rows, :]), reads=[Rpb], writes=[Ryt[q]])
                                else:
                                    S.op("dve", lambda e: e.tensor_copy(YT[0:rows, 512:1024], pb[0:rows, :]), reads=[Rpb], writes=[Ryt[q]])
                            if ti < 4:
                                out_dma(y_p[c0 + a * 128:c0 + (a + 1) * 128, :], YT[:, :], [Ryt[q]], f"yt{q}")
                            else:
                                for t in range(4):
                                    out_dma(y_s[:, t, :], YT[16 * t:16 * t + 16, :], [Ryt[q]], f"yt{q}")
                    Tq[ti] = T_stage
                Tq = {}
                return final_cb, Tq

            xs_v, Rxs = AR.carve("xstage", 2816, 2 * 4096, 2)
            xst = [xs_v[:, 0:4096].rearrange("p (a d) -> p a d", a=4), xs_v[:, 4096:8192].rearrange("p (a d) -> p a d", a=4)]
            dxs = [S.dsem("dxs0"), S.dsem("dxs1")]
            nctx0 = norm_begin()
            for ti, (c0, n) in enumerate(TTS):
                s_ = ti % 2
                if ti < 4:
                    S.dma("sp", xst[s_], xp[c0:c0 + 512, :].rearrange("(a p) d -> p a d", p=128), dxs[s_], writes=[Rxs[s_]])
                    na, rows = 4, 128
                else:
                    for t in range(4):
                        S.dma("sp", xst[s_][16 * t:16 * t + 16, 0, :], xs[:, t, :], dxs[s_], writes=[Rxs[s_]])
                    na, rows = 1, 64
                for kc in range(KC):
                    pb, Rpb = bank()
                    for a in range(na):
                        S.op("pe", lambda e: e.transpose(pb[:, a * 128:a * 128 + rows], xst[s_][0:rows, a, kc * 128:(kc + 1) * 128],
                                                         ident[0:rows, 0:rows]),
                             reads=[Rxs[s_], Rcst], writes=[Rpb], inc=(a == na - 1))
                    eng = "act" if kc % 2 == 0 else "dve"
                    if eng == "act":
                        S.op("act", lambda e: e.copy(X[:, kc, c0:c0 + n], pb[:, 0:n]), reads=[Rpb], writes=[RX[kc][ti]])
                    else:
                        S.op("dve", lambda e: e.tensor_copy(X[:, kc, c0:c0 + n], pb[:, 0:n]), reads=[Rpb], writes=[RX[kc][ti]])
                if ti >= 1:
                    norm_tt(nctx0, VT_N1, ti - 1)
            norm_tt(nctx0, VT_N1, 4)

            for l in range(DEPTH):
                Win = w_in[l].rearrange("(k p) n -> p k n", p=128)
                Wout = w_out[l]
                Wup = w_up[l].rearrange("(k p) n -> p k n", p=128)
                Wdn = w_down[l].rearrange("(k p) n -> p k n", p=128)

                if l == 0:
                    prep_dma(0)
                    prep(0)
                mark(f'prep{l}')

                mark(f'n1_{l}')
                CPW = 30 + NP_
                CSW = 34 * 16
                CW_ = CPW + CSW
                BB_ = AW - 9504
                AB_ = AW - 5280
                CB_ = AW - 3168
                cin_v, Rcin = AR.carve("cin", 0, 3 * CW_ // 2, 3 * 6)
                cin = cin_v.bitcast(BF16).rearrange("p (j n) -> p j n", j=3)
                d0_ = 3 * CW_ // 2
                dg_v, Rdg = AR.carve("dg", d0_, 3 * 31 * 64, 3)
                dgb = dg_v.bitcast(BF16).rearrange("p (b i n) -> p b i n", b=3, i=31)
                a1 = d0_ + 3 * 31 * 64
                sig_v, Rsig = AR.carve("sig", a1, 2 * 512, 2)
                a2 = a1 + 1024
                cst_v, Rcstg = AR.carve("cstg", a2, 384, 1)

                def RC(j, k):
                    return Rcin[j * 6 + k]

                for j in range(3):
                    S.op("dve", lambda e: e.memset(cin[:, j, 0:30], 0.0), writes=[RC(j, 0)])
                def cst_dma(q):
                    S.dma("sp", STG[q % 2][0:120, 0:384], sc_[l, 4 * q:4 * q + 4].rearrange("b i c -> (b i) c"), DSTG[q % 2], writes=[RSTG[q % 2]])

                def cst_tr(q):
                    s_ = q % 2
                    for j2 in range(3):
                        pb, Rpb = bank()
                        S.op("pe", lambda e: e.transpose(pb[:, 0:120], STG[s_][0:120, j2 * 128:(j2 + 1) * 128], ident[0:120, 0:120]),
                             reads=[RSTG[s_], Rcst], writes=[Rpb])
                        dst = cin[:, j2, CPW:CPW + 480].rearrange("p (i b) -> p i b", b=16)[:, :, 4 * q:4 * q + 4]
                        srcv = pb[:, 0:120].rearrange("p (b i) -> p i b", i=30)
                        S.op("act", lambda e: e.copy(dst, srcv), reads=[Rpb], writes=[RC(j2, 5)])
                cst_dma(0)
                cst_dma(1)
                cslabs = []
                for j in range(3):
                    sl = wslot()
                    wv = WS[sl][:, 0:2048].rearrange("p (k n) -> p k n", k=8)
                    wdma(sl, wv[:, :, 0:128], Win[:, :, 1024 + 128 * j:1024 + 128 * j + 128])
                    wdma(sl, wv[:, :, 128:256], Win[:, :, 1408 + 128 * j:1408 + 128 * j + 128])
                    cslabs.append((sl, wv))
                for j in range(3):
                    for i in range(31):
                        wcol = VT[:, VT_CW + (l * 31 + i) * 3 + j:VT_CW + (l * 31 + i) * 3 + j + 1]
                        S.op("pool", lambda e: e.tensor_scalar(dgb[:, j, i, :], IDB[:], wcol, 0.0, ALU.mult, ALU.add),
                             reads=[Rvt, Rones], writes=[Rdg[j]])
                out_dma(o_conv_s[l][:, 0:26, :], sc_[l][:, 4:30, :], [], "d2d")
                for j in range(3):
                    sl, wv = cslabs[j]
                    for ti, (c0, n) in enumerate(TTS):
                        pg, Rpg = bank()
                        for k in range(KC):
                            S.op("pe", lambda e: e.matmul(pg[:, 0:n], wv[:, k, 128:256], H[:, k, c0:c0 + n], start=(k == 0), stop=(k == 7)),
                                 reads=[RWS[sl], RH[ti]], writes=[Rpg], inc=(k == 7))
                        pv, Rpv = bank()
                        for k in range(KC):
                            S.op("pe", lambda e: e.matmul(pv[:, 0:n], wv[:, k, 0:128], H[:, k, c0:c0 + n], start=(k == 0), stop=(k == 7)),
                                 reads=[RWS[sl], RH[ti]], writes=[Rpv], inc=(k == 7))
                        b = ti % 2
                        sg = sig_v[:, b * 512:b * 512 + n]
                        S.op("act", lambda e: e.activation(sg, pg[:, 0:n], AF.Sigmoid), reads=[Rpg], writes=[Rsig[b]])
                        if ti < 4:
                            dst = cin[:, j, 30 + c0:30 + c0 + n]
                        else:
                            dst = cin[:, j, CPW + 480:CPW + 544]
                        S.op("dve", lambda e: e.tensor_tensor(dst, pv[:, 0:n], sg, ALU.mult), reads=[Rpv, Rsig[b]], writes=[RC(j, 1 + ti)])
                    if j == 0:
                        cst_tr(0)
                        cst_tr(1)
                        cst_dma(2)
                        cst_dma(3)
                    elif j == 1:
                        cst_tr(2)
                        cst_tr(3)
                    pt, Rpt = bank()
                    for k in range(KC):
                        S.op("pe", lambda e: e.matmul(pt[0:94, 0:256], H[:, k, NT - 94:NT], wv[:, k, 0:256], start=(k == 0), stop=(k == 7)),
                             reads=[RWS[sl], RH[3], RH[4]], writes=[Rpt], inc=(k == 7))
                    sg = sig_v[0:94, 0:128]
                    S.op("act", lambda e: e.activation(sg, pt[0:94, 128:256], AF.Sigmoid), reads=[Rpt], writes=[Rsig[0]])
                    S.op("dve", lambda e: e.tensor_tensor(cst_v[0:94, j * 128:(j + 1) * 128], pt[0:94, 0:128], sg, ALU.mult),
                         reads=[Rpt, Rsig[0]], writes=[Rcstg[0]])
                out_dma(o_conv_p[l], cst_v[0:30, 0:384], [Rcstg[0]], "cstg")
                for t in range(4):
                    out_dma(o_conv_s[l][:, 26 + t, :], cst_v[30 + 16 * t:46 + 16 * t, 0:384], [Rcstg[0]], "cstg")

                mark(f'cin{l}')
                mark(f'conv{l}')
                ln0_ = a1
                ybf_v, Rybf = AR.carve("ybf", ln0_, 4 * 256, 4)
                ybf = ybf_v.bitcast(BF16).rearrange("p (a n) -> p a n", a=4)
                lt_v, Rlt = AR.carve("lnt", ln0_ + 1024, 4 * 512, 4)
                yt_v2, Ryt2 = AR.carve("lnyt", ln0_ + 1024 + 2048, 2 * 512, 2)
                assert ln0_ + 1024 + 2048 + 1024 <= CB_, (ln0_, CB_)
                cout_v, Rco = AR.carve("cout", CB_, 3 * NT // 2, 5)
                cout = cout_v.bitcast(BF16).rearrange("p (j n) -> p j n", j=3)
                for pos, ti in enumerate([4, 0, 1, 2, 3]):
                    c0, n = TTS[ti]
                    pjs = []
                    for j in range(3):
                        pb, Rpb = bank()
                        pjs.append((pb, Rpb))
                        for i in range(31):
                            if ti < 4:
                                src = cin[:, j, i + c0:i + c0 + n]
                                rr = [RC(j, k) for k in range(5)]
                            else:
                                src = cin[:, j, CPW + 16 * i:CPW + 16 * i + 64]
                                rr = [RC(j, 5)]
                            S.op("pe", lambda e: e.matmul(pb[:, 0:n], dgb[:, j, i, :], src, start=(i == 0), stop=(i == 30)),
                                 reads=rr + [Rdg[j]], writes=[Rpb], inc=(i == 30))
                    pm, Rpm = bank()
                    pq, Rpq = bank()
                    for j in range(3):
                        pb, Rpb = pjs[j]
                        cbias = VT[:, VT_CB + l * 3 + j:VT_CB + l * 3 + j + 1]
                        q0, q1 = (2 * j) % 4, (2 * j + 1) % 4
                        S.op("act", lambda e: e.activation(ybf[:, q0, 0:n], pb[:, 0:n], AF.Identity, bias=cbias), reads=[Rpb, Rvt], writes=[Rybf[q0]])
                        S.op("pe", lambda e: e.matmul(pm[:, 0:n], ONES[:], ybf[:, q0, 0:n], start=(j == 0), stop=(j == 2)),
                             reads=[Rybf[q0], Rones], writes=[Rpm])
                        S.op("act", lambda e: e.activation(ybf[:, q1, 0:n], pb[:, 0:n], AF.Square, bias=cbias), reads=[Rpb, Rvt], writes=[Rybf[q1]])
                        S.op("pe", lambda e: e.matmul(pq[:, 0:n], ONES[:], ybf[:, q1, 0:n], start=(j == 0), stop=(j == 2)),
                             reads=[Rybf[q1], Rones], writes=[Rpq])
                    b = (pos % 2) * 2
                    mu = lt_v[:, b * 512:b * 512 + n]
                    rs = lt_v[:, (b + 1) * 512:(b + 1) * 512 + n]
                    S.op("act", lambda e: e.mul(mu, pm[:, 0:n], 1.0 / D_C), reads=[Rpm], writes=[Rlt[b]])
                    S.op("dve", lambda e: e.tensor_tensor(rs, mu, mu, ALU.mult), reads=[Rlt[b]], writes=[Rlt[b + 1]])
                    S.op("dve", lambda e: e.scalar_tensor_tensor(rs, pq[:, 0:n], 1.0 / D_C, rs, ALU.mult, ALU.subtract),
                         reads=[Rpq, Rlt[b + 1]], writes=[Rlt[b + 1]])
                    S.op("dve", lambda e: e.tensor_scalar(rs, rs, 0.0, None, ALU.max), reads=[Rlt[b + 1]], writes=[Rlt[b + 1]])
                    S.op("act", lambda e: e.activation(rs, rs, AF.Ln, bias=EPST[:, 0:1], scale=1.0), reads=[Rlt[b + 1], Rones], writes=[Rlt[b + 1]])
                    S.op("act", lambda e: e.activation(rs, rs, AF.Exp, scale=-0.5), reads=[Rlt[b + 1]], writes=[Rlt[b + 1]])
                    for j in range(3):
                        pb, Rpb = pjs[j]
                        cbias = VT[:, VT_CB + l * 3 + j:VT_CB + l * 3 + j + 1]
                        yq = (ti * 3 + j) % 2
                        a_ = yt_v2[:, yq * 512:yq * 512 + n]
                        S.op("dve", lambda e: e.scalar_tensor_tensor(a_, pb[:, 0:n], cbias, mu, ALU.add, ALU.subtract),
                             reads=[Rpb, Rlt[b], Rvt], writes=[Ryt2[yq]])
                        S.op("dve", lambda e: e.tensor_tensor(a_, a_, rs, ALU.mult), reads=[Rlt[b + 1]], writes=[Ryt2[yq]])
                        gcol = VT[:, VT_CG + l * 3 + j:VT_CG + l * 3 + j + 1]
                        bcol = VT[:, VT_CBB + l * 3 + j:VT_CBB + l * 3 + j + 1]
                        S.op("act", lambda e: e.activation(cout[:, j, c0:c0 + n], a_, AF.Silu, bias=bcol, scale=gcol),
                             reads=[Ryt2[yq], Rvt], writes=[Rco[ti]])

                mark(f'ln{l}')
                PPW = 15 + NP_
                PSW = 19 * 16
                ZW = PPW + PSW
                aout_v, Rao = AR.carve("aout", AB_, NT, 5)
                aout = aout_v.bitcast(BF16).rearrange("p (j n) -> p j n", j=2)
                b1 = 0
                zab_v, Rzab = AR.carve("zab", b1, ZW, 14)
                zab = zab_v.bitcast(BF16).rearrange("p (j n) -> p j n", j=2)
                wp_v, Rwp = AR.carve("wp", b1 + ZW, 2 * 16 * 64, 2)
                wpb = wp_v.bitcast(BF16).rearrange("p (j i n) -> p j i n", j=2, i=16)
                sm_v, Rsm = AR.carve("poolsm", b1 + ZW + 2048, 136, 4)
                pstg_v, Rpstg = AR.carve("pstg", b1 + ZW + 2048 + 136, 256, 1)
                fst_v, Rfst = AR.carve("fst", b1 + ZW + 2048 + 136 + 256, 2816, 1)
                sl = wslot()
                wv = WS[sl][:, 0:2048].rearrange("p (k n) -> p k n", k=8)
                wdma(sl, wv, Win[:, :, 0:256])
                for q in range(2):
                    S.dma("sp", STG[q][0:120, 0:256], sp_[l, 8 * q:8 * q + 8].rearrange("b i c -> (b i) c"), DSTG[q], writes=[RSTG[q]])
                S.dma("sp", fst_v[0:32, 0:2816], sf_[l].rearrange("b i c -> (b i) c"), DM["fst"], writes=[Rfst[0]])
                out_dma(o_pool_s[l][:, 0:11, :], sp_[l][:, 4:15, :], [], "d2d")
                WMAX = [4, 16]
                for j in range(2):
                    for i in range(WMAX[j]):
                        S.op("pool", lambda e: e.tensor_scalar(wpb[:, j, i, :], PWf[:, j, :], CST[:, C_CT + 16 * j + i:C_CT + 16 * j + i + 1], 0.0, ALU.mult, ALU.add),
                             reads=[Rpwf, Rcst], writes=[Rwp[j]])
                zsA = [sm_v[:, 0:31], sm_v[:, 31:62]]
                RzsA = [Rsm[0], Rsm[1]]
                for j in range(2):
                    RZ = Rzab[7 * j:7 * j + 7]
                    zs = zsA[j]
                    S.op("dve", lambda e: e.memset(zab[:, j, 0:15], 0.0), writes=[RZ[0]])
                    S.op("dve", lambda e: e.memset(zs[:, 0:15], 0.0), writes=[RzsA[j]])
                    for ti, (c0, n) in enumerate(TTS):
                        pb, Rpb = bank()
                        for k in range(KC):
                            S.op("pe", lambda e: e.matmul(pb[:, 0:n], wv[:, k, j * 128:(j + 1) * 128], H[:, k, c0:c0 + n], start=(k == 0), stop=(k == 7)),
                                 reads=[RWS[sl], RH[ti]], writes=[Rpb], inc=(k == 7))
                        dst = zab[:, j, 15 + c0:15 + c0 + n] if ti < 4 else zab[:, j, PPW + 240:PPW + 304]
                        S.op("act", lambda e: e.copy(dst, pb[:, 0:n]), reads=[Rpb], writes=[RZ[1 + ti]])
                        if ti == 0:
                            S.op("dve", lambda e: e.tensor_copy(zs[:, 15:31], pb[:, 0:16]), reads=[Rpb], writes=[RzsA[j]])
                for j in range(2):
                    RZ = Rzab[7 * j:7 * j + 7]
                    zs = zsA[j]
                    ssa = sm_v[:, 62:93]
                    ssb = sm_v[:, 93:124]
                    pf = sm_v[:, 124:132].bitcast(BF16)
                    for q in range(2):
                        s_ = q % 2
                        pb, Rpb = bank()
                        S.op("pe", lambda e: e.transpose(pb[:, 0:120], STG[s_][0:120, j * 128:(j + 1) * 128], ident[0:120, 0:120]),
                             reads=[RSTG[s_], Rcst], writes=[Rpb])
                        dst = zab[:, j, PPW:PPW + 240].rearrange("p (i b) -> p i b", b=16)[:, :, 8 * q:8 * q + 8]
                        S.op("act", lambda e: e.copy(dst, pb[:, 0:120].rearrange("p (b i) -> p i b", i=15)), reads=[Rpb], writes=[RZ[6]])
                    for ti, (c0, n) in enumerate(TTS):
                        pb, Rpb = bank()
                        for i in range(WMAX[j]):
                            if ti < 4:
                                src = zab[:, j, 15 + c0 - i:15 + c0 - i + n]
                            else:
                                src = zab[:, j, PPW + 240 - 16 * i:PPW + 304 - 16 * i]
                            S.op("pe", lambda e: e.matmul(pb[:, 0:n], wpb[:, j, i, :], src, start=(i == 0), stop=(i == WMAX[j] - 1)),
                                 reads=([RZ[ti], RZ[1 + ti]] if ti < 4 else [RZ[5], RZ[6]]) + [Rwp[j]], writes=[Rpb], inc=(i == WMAX[j] - 1))
                        S.op("act", lambda e: e.activation(aout[:, j, c0:c0 + n], pb[:, 0:n], AF.Identity,
                                                           scale=VT[:, VT_PS + l * 2 + j:VT_PS + l * 2 + j + 1]),
                             reads=[Rpb, Rvt], writes=[Rao[ti]])
                    def wsum_s(dst_v, src_v, sh, p0, p1, rd, wr):
                        S.op("dve", lambda e: e.tensor_tensor(dst_v[p0:p1, sh:31], src_v[p0:p1, sh:31], src_v[p0:p1, 0:31 - sh], ALU.add),
                             reads=rd, writes=wr)
                    R0, R1, R2, R3 = [RzsA[j]], [Rsm[2]], [Rsm[3]], [Rsm[3]]
                    if j == 0:
                        wsum_s(ssa, zs, 1, 64, 128, R0, R1)
                        wsum_s(ssb, zs, 1, 0, 64, R0, R2)
                        wsum_s(ssb, ssa, 2, 64, 128, R1, R2)
                    else:
                        wsum_s(ssa, zs, 1, 0, 128, R0, R1)
                        wsum_s(ssb, ssa, 2, 0, 128, R1, R2)
                        wsum_s(ssa, ssb, 4, 0, 128, R2, R1)
                        wsum_s(ssb, ssa, 8, 64, 128, R1, R2)
                        S.op("dve", lambda e: e.tensor_copy(ssb[0:64, 15:31], ssa[0:64, 15:31]), reads=R1, writes=R2)
                    ic = CST[:, C_INVC + 16 * j:C_INVC + 16 * j + 16]
                    S.op("dve", lambda e: e.tensor_tensor(ssb[:, 15:31], ssb[:, 15:31], ic, ALU.mult), reads=[Rcst], writes=R2)
                    S.op("dve", lambda e: e.tensor_tensor(pf[:, 0:16], ssb[:, 15:31], zs[:, 15:31], ALU.subtract), reads=R2 + R0, writes=R3)
                    pb, Rpb = bank()
                    S.op("pe", lambda e: e.matmul(pb[:, 0:16], PW[:, j, :], pf[:, 0:16], start=True, stop=True), reads=R3 + [Rpw], writes=[Rpb])
                    S.op("act", lambda e: e.activation(aout[:, j, 0:16], pb[:, 0:16], AF.Identity,
                                                       scale=VT[:, VT_PS + l * 2 + j:VT_PS + l * 2 + j + 1]),
                         reads=[Rpb, Rvt], writes=[Rao[0]])
                pt, Rpt = bank()
                for k in range(KC):
                    S.op("pe", lambda e: e.matmul(pt[0:79, 0:256], H[:, k, NT - 79:NT], wv[:, k, 0:256], start=(k == 0), stop=(k == 7)),
                         reads=[RWS[sl], RH[3], RH[4]], writes=[Rpt], inc=(k == 7))
                S.op("act", lambda e: e.copy(pstg_v[0:79, 0:256], pt[0:79, 0:256]), reads=[Rpt], writes=[Rpstg[0]])
                out_dma(o_pool_p[l], pstg_v[0:15, 0:256], [Rpstg[0]], "pstg")
                for t in range(4):
                    out_dma(o_pool_s[l][:, 11 + t, :], pstg_v[15 + 16 * t:31 + 16 * t, 0:256], [Rpstg[0]], "pstg")

                for f in range(NFF):
                    pb, Rpb = bank()
                    S.op("pe", lambda e: e.transpose(pb[:, 0:32], fst_v[0:32, f * 128:(f + 1) * 128], ident[0:32, 0:32]),
                         reads=[Rfst[0], Rcst], writes=[Rpb])
                    S.op("act", lambda e: e.copy(FS[:, f, :].rearrange("p (i b) -> p i b", b=16), pb[:, 0:32].rearrange("p (b i) -> p i b", i=2)),
                         reads=[Rpb], writes=[Rfs])
                mark(f'ga{l}')
                bout_v, Rbo = AR.carve("bout", BB_, 2 * NT, 5)
                bout = bout_v.bitcast(BF16).rearrange("p (h n) -> p h n", h=4)
                c1_ = 0
                vb_v, Rvb = AR.carve("vb", c1_, 17 * 192, 17)
                vb = vb_v.bitcast(BF16).rearrange("p (c n) -> p c n", c=17)
                c2_ = c1_ + 17 * 192
                tt_v, Rtt = AR.carve("ttmp", c2_, 2 * 512, 2)
                vs_v, Rvs = AR.carve("vstg", c2_ + 1024, 384, 1)
                slv = wslot()
                wvv = WS[slv][:, 0:3072].rearrange("p (k n) -> p k n", k=8)
                wdma(slv, wvv, Win[:, :, 640:1024])
                slu = wslot()
                wvu = WS[slu][:, 0:3072].rearrange("p (k n) -> p k n", k=8)
                wdma(slu, wvu, Win[:, :, 256:640])
                mark(f'gbva{l}')
                for c in range(17):
                    if c == 1:
                        mark(f'gbvb{l}')
                    if c == 16:
                        mark(f'gbvc{l}')
                    rows = 128 if c < 16 else 64
                    ti = c // 4
                    pb, Rpb = bank()
                    for k in range(KC):
                        S.op("pe", lambda e: e.matmul(pb[0:rows, 0:384], H[:, k, c * 128:c * 128 + rows], wvv[:, k, :], start=(k == 0), stop=(k == 7)),
                             reads=[RWS[slv], RH[ti]], writes=[Rpb], inc=(k == 7))
                    S.op("act", lambda e: e.copy(vb[0:rows, c, :], pb[0:rows, 0:384]), reads=[Rpb], writes=[Rvb[c]])
                    if c == 16:
                        S.op("dve", lambda e: e.tensor_copy(vs_v[0:64, 0:384], pb[0:64, 0:384]), reads=[Rpb], writes=[Rvs[0]])
                        for t in range(4):
                            out_dma(o_v_s[l][:, t, :], vs_v[16 * t:16 * t + 16, 0:384], [Rvs[0]], "vstg")
                mark(f'gbv{l}')
                for h in range(4):
                    for ti, (c0, n) in enumerate(TTS):
                        if ti == 4:
                            mark(f'gbh{l}')
                        pS, RpS = bank()
                        b = ti % 2
                        T_ = tt_v[0:96, b * 512:b * 512 + n]
                        if ti < 4:
                            for cc in range(4):
                                c = ti * 4 + cc
                                S.op("pe", lambda e: e.matmul(pS[0:96, cc * 128:(cc + 1) * 128], vb[:, c, h * 96:(h + 1) * 96], WT[:, h, :],
                                                              start=True, stop=True, skip_group_check=True),
                                     reads=[Rvb[c], Rwt], writes=[RpS], inc=(cc == 3))
                            S.op("dve", lambda e: e.tensor_tensor(T_.rearrange("p (c n) -> p c n", c=4), pS[0:96, 0:512].rearrange("p (c n) -> p c n", c=4),
                                                                  BB[:, h, 0:128].unsqueeze(1).broadcast_to([96, 4, 128]), ALU.add),
                                 reads=[RpS, Rbb], writes=[Rtt[b]])
                        else:
                            S.op("pe", lambda e: e.matmul(pS[0:96, 0:64], vb[0:64, 16, h * 96:(h + 1) * 96], WTS[:, h, :],
                                                          start=True, stop=True),
                                 reads=[Rvb[16], Rwts], writes=[RpS])
                            S.op("dve", lambda e: e.tensor_tensor(T_, pS[0:96, 0:64], BB[:, h, 128:192], ALU.add),
                                 reads=[RpS, Rbb], writes=[Rtt[b]])
                        pU, RpU = bank()
                        for k in range(KC):
                            S.op("pe", lambda e: e.matmul(pU[0:96, 0:n], wvu[:, k, h * 96:(h + 1) * 96], H[:, k, c0:c0 + n], start=(k == 0), stop=(k == 7)),
                                 reads=[RWS[slu], RH[ti]], writes=[RpU], inc=(k == 7))
                        S.op("dve", lambda e: e.tensor_tensor(bout[0:96, h, c0:c0 + n], pU[0:96, 0:n], T_, ALU.mult),
                             reads=[RpU, Rtt[b]], writes=[Rbo[ti]])

                mark(f'gb{l}')
                for mm in range(4):
                    sl = wslot()
                    wo = WS[sl][:, 0:2304].rearrange("p (k n) -> p k n", k=9)
                    cs = slice(mm * 256, mm * 256 + 256)
                    wdma(sl, wo[:, 0:2, :], Wout[0:256, cs].rearrange("(k p) n -> p k n", p=128))
                    wdma(sl, wo[0:96, 2:6, :], Wout[256:640, cs].rearrange("(k p) n -> p k n", p=96))
                    wdma(sl, wo[:, 6:9, :], Wout[640:1024, cs].rearrange("(k p) n -> p k n", p=128))
                    def oproj_block(m2, ti):
                        m = mm * 2 + m2
                        ms = slice(m2 * 128, m2 * 128 + 128)
                        c0, n = TTS[ti]
                        pb, Rpb = bank()
                        ops = []
                        for j in range(2):
                            ops.append((wo[:, j, ms], aout[:, j, c0:c0 + n], Rao[ti]))
                        for h in range(4):
                            ops.append((wo[0:96, 2 + h, ms], bout[0:96, h, c0:c0 + n], Rbo[ti]))
                        for j in range(3):
                            ops.append((wo[:, 6 + j, ms], cout[:, j, c0:c0 + n], Rco[ti]))
                        for q, (lh, rh, rr) in enumerate(ops):
                            S.op("pe", lambda e: e.matmul(pb[:, 0:n], lh, rh, start=(q == 0), stop=(q == 8)),
                                 reads=[RWS[sl], rr], writes=[Rpb], inc=(q == 8))
                        S.op("dve", lambda e: e.tensor_tensor(X[:, m, c0:c0 + n], pb[:, 0:n], X[:, m, c0:c0 + n], ALU.add),
                             reads=[Rpb], writes=[RX[m][ti]])

                    if mm < 3:
                        for m2 in range(2):
                            for ti in range(5):
                                oproj_block(m2, ti)
                    else:
                        nctx2 = norm_begin()
                        o_order = [4, 0, 1, 2, 3]
                        for pos, ti in enumerate(o_order):
                            for m2 in range(2):
                                oproj_block(m2, ti)
                            if pos >= 1:
                                norm_tt(nctx2, VT_N2 + l * 8, o_order[pos - 1])
                        norm_tt(nctx2, VT_N2 + l * 8, o_order[4])

                mark(f'op{l}')
                if l + 1 < DEPTH:
                    prep_dma(l + 1)
                mark(f'n2_{l}')
                GPW = 2 + NP_
                GSW = 6 * 16
                GW = GPW + GSW
                nb0 = 768 + 2048
                g_v, Rg = AR.carve("graw", nb0, 2 * GW, 14)
                tm_v, Rtm = AR.carve("ftmp", nb0 + 2 * GW, 3 * 512, 3)
                fo_v, Rfo = AR.carve("fostg", nb0 + 2 * GW + 1536, 2 * 128, 2)
                ab0 = nb0 + 2 * GW + 1536 + 256
                actb_v, Rab = AR.carve("actb", ab0, 8 * NT // 2, 5)
                actb = actb_v.bitcast(BF16).rearrange("p (f n) -> p f n", f=8)
                vst_v, Rvst = AR.carve("valst", ab0 + 8 * NT // 2, 3 * 256, 3)
                vstb = vst_v.bitcast(BF16).rearrange("p (a n) -> p a n", a=3)
                tmi = 0
                pend = {"f": None}
                for (f0, nf) in FF_GROUPS:
                    for fl in range(nf):
                        f = f0 + fl
                        sl = wslot()
                        wv = WS[sl][:, 0:2048].rearrange("p (k n) -> p k n", k=8)
                        wdma(sl, wv[:, :, 0:128], Wup[:, :, f * 128:(f + 1) * 128])
                        wdma(sl, wv[:, :, 128:256], Wup[:, :, D_FF + f * 128:D_FF + (f + 1) * 128])
                        gb = f % 2
                        G = g_v[:, gb * GW:(gb + 1) * GW]
                        S.op("dve", lambda e: e.memset(G[:, 0:2], 0.0), writes=[Rg[7 * gb]])
                        S.op("dve", lambda e: e.tensor_copy(G[:, GPW:GPW + 32], FS[:, f, :]), reads=[Rfs], writes=[Rg[7 * gb + 5]])
                        w0 = VT[:, VT_FW + (l * 3 + 0) * NFF + f:VT_FW + (l * 3 + 0) * NFF + f + 1]
                        w1 = VT[:, VT_FW + (l * 3 + 1) * NFF + f:VT_FW + (l * 3 + 1) * NFF + f + 1]
                        w2 = VT[:, VT_FW + (l * 3 + 2) * NFF + f:VT_FW + (l * 3 + 2) * NFF + f + 1]
                        fb = VT[:, VT_FB + l * NFF + f:VT_FB + l * NFF + f + 1]
                        for ti, (c0, n) in enumerate(TTS):
                            pg, Rpg = bank()
                            for k in range(KC):
                                S.op("pe", lambda e: e.matmul(pg[:, 0:n], wv[:, k, 0:128], H[:, k, c0:c0 + n], start=(k == 0), stop=(k == 7)),
                                     reads=[RWS[sl], RH[ti]], writes=[Rpg], inc=(k == 7))
                            pv, Rpv = bank()
                            for k in range(KC):
                                S.op("pe", lambda e: e.matmul(pv[:, 0:n], wv[:, k, 128:256], H[:, k, c0:c0 + n], start=(k == 0), stop=(k == 7)),
                                     reads=[RWS[sl], RH[ti]], writes=[Rpv], inc=(k == 7))
                            if ti < 4:
                                gc, sh = 2 + c0, 1
                            else:
                                gc, sh = GPW + 32, 16
                            gw_ = Rg[7 * gb + 1 + ti] if ti < 4 else Rg[7 * gb + 6]
                            gr_ = [Rg[7 * gb + ti], Rg[7 * gb + 1 + ti]] if ti < 4 else [Rg[7 * gb + 5], Rg[7 * gb + 6]]
                            S.op("act", lambda e: e.copy(G[:, gc:gc + n], pg[:, 0:n]), reads=[Rpg], writes=[gw_])
                            q = tmi % 3
                            tmi += 1
                            T_ = tm_v[:, q * 512:q * 512 + n]
                            S.op("act", lambda e: e.activation(T_, pg[:, 0:n], AF.Identity, bias=fb, scale=w2), reads=[Rpg, Rvt], writes=[Rtm[q]])
                            S.op("dve", lambda e: e.scalar_tensor_tensor(T_, G[:, gc - sh:gc - sh + n], w1, T_, ALU.mult, ALU.add),
                                 reads=gr_ + [Rvt], writes=[Rtm[q]])
                            S.op("dve", lambda e: e.scalar_tensor_tensor(T_, G[:, gc - 2 * sh:gc - 2 * sh + n], w0, T_, ALU.mult, ALU.add),
                                 reads=gr_ + [Rvt], writes=[Rtm[q]])
                            VB_ = vstb[:, q, 0:n]
                            S.op("act", lambda e: e.copy(VB_, pv[:, 0:n]), reads=[Rpv], writes=[Rvst[q]])

                            def stage2(T_=T_, q=q, VB_=VB_, fl=fl, c0=c0, n=n, ti=ti):
                                S.op("act", lambda e: e.activation(T_, T_, AF.Silu), reads=[Rtm[q]], writes=[Rtm[q]])
                                S.op("dve", lambda e: e.tensor_tensor(actb[:, fl, c0:c0 + n], VB_, T_, ALU.mult),
                                     reads=[Rvst[q], Rtm[q]], writes=[Rab[ti]])
                            if pend["f"] is not None:
                                pend["f"]()
                            pend["f"] = stage2
                        pt, Rpt = bank()
                        for k in range(KC):
                            S.op("pe", lambda e: e.matmul(pt[0:66, 0:128], H[:, k, NT - 66:NT], wv[:, k, 0:128], start=(k == 0), stop=(k == 7)),
                                 reads=[RWS[sl], RH[3], RH[4]], writes=[Rpt], inc=(k == 7))
                        ob = f % 2
                        S.op("act", lambda e: e.copy(fo_v[0:66, ob * 128:(ob + 1) * 128], pt[0:66, 0:128]), reads=[Rpt], writes=[Rfo[ob]])
                        out_dma(o_ffn_p[l][:, f * 128:(f + 1) * 128], fo_v[0:2, ob * 128:(ob + 1) * 128], [Rfo[ob]], f"fo{ob}")
                        for i in range(2):
                            out_dma(o_ffn_s[l][:, i, f * 128:(f + 1) * 128], fo_v[2 + 16 * (2 + i):18 + 16 * (2 + i), ob * 128:(ob + 1) * 128],
                                    [Rfo[ob]], f"fo{ob}")
                    if pend["f"] is not None:
                        pend["f"]()
                        pend["f"] = None
                    last = (f0 + nf == NFF)
                    if not last:
                        for mm in range(4):
                            sl = wslot()
                            wd = WS[sl][:, 0:nf * 256].rearrange("p (k n) -> p k n", k=nf)
                            wdma(sl, wd, Wdn[:, f0:f0 + nf, mm * 256:(mm + 1) * 256])
                            for m2 in range(2):
                                m = mm * 2 + m2
                                for ti, (c0, n) in enumerate(TTS):
                                    pb, Rpb = bank()
                                    for kk in range(nf):
                                        S.op("pe", lambda e: e.matmul(pb[:, 0:n], wd[:, kk, m2 * 128:(m2 + 1) * 128], actb[:, kk, c0:c0 + n],
                                                                      start=(kk == 0), stop=(kk == nf - 1)),
                                             reads=[RWS[sl], Rab[ti]], writes=[Rpb], inc=(kk == nf - 1))
                                    S.op("dve", lambda e: e.tensor_tensor(X[:, m, c0:c0 + n], pb[:, 0:n], X[:, m, c0:c0 + n], ALU.add),
                                         reads=[Rpb], writes=[RX[m][ti]])
                    else:
                        wparts = []
                        for k0 in range(0, nf, 3):
                            nk = min(3, nf - k0)
                            sl = wslot()
                            wvw = WS[sl][:, 0:nk * 1024].rearrange("p (k n) -> p k n", k=nk)
                            wdma(sl, wvw, Wdn[:, f0 + k0:f0 + k0 + nk, :])
                            wparts.append((sl, wvw, k0))
                        if l + 1 < DEPTH:
                            prep(l + 1)
                        nctx = norm_begin()
                        is_fin = (l + 1 == DEPTH)
                        if is_fin:
                            fcb, Tq = make_final_cb()

                        tails = {}

                        def head(ti):
                            if not is_fin:
                                tails[ti] = norm_tt(nctx, VT_N1 + (l + 1) * 8, ti, defer=True)
                            else:
                                tails[ti] = norm_tt(nctx, VT_FN, ti, final=True, ycb=fcb, defer=True)

                        order = [4, 0, 1, 2, 3]
                        for pos, ti in enumerate(order):
                            c0, n = TTS[ti]
                            for m in range(8):
                                pb, Rpb = bank()
                                for kk in range(nf):
                                    sl, wvw, k0 = wparts[kk // 3]
                                    S.op("pe", lambda e: e.matmul(pb[:, 0:n], wvw[:, kk - k0, m * 128:(m + 1) * 128], actb[:, kk, c0:c0 + n],
                                                                  start=(kk == 0), stop=(kk == nf - 1)),
                                         reads=[RWS[sl], Rab[ti]], writes=[Rpb], inc=(kk == nf - 1))
                                S.op("dve", lambda e: e.tensor_tensor(X[:, m, c0:c0 + n], pb[:, 0:n], X[:, m, c0:c0 + n], ALU.add),
                                     reads=[Rpb], writes=[RX[m][ti]])
                            if pos >= 1:
                                head(order[pos - 1])
                            if is_fin and pos >= 3:
                                Tq[order[pos - 3]]()
                            if pos >= 2:
                                tails[order[pos - 2]]()
                        head(order[4])
                        if is_fin:
                            Tq[order[2]]()
                        tails[order[3]]()
                        if is_fin:
                            Tq[order[3]]()
                        tails[order[4]]()
                        if is_fin:
                            Tq[order[4]]()

            mark('layers')

        except _Stop:
            pass

        sp = S.E["sp"]["eng"]
        for d in DOUT.values():
            if d.cnt:
                sp.wait_ge(d.sem, d.cnt)
        for d in S.dsems:
            assert d.cnt < 60000, d.cnt
        for e in S.E.values():
            assert e["cnt"] < 60000
    return nc


def _consts():
    c = np.zeros((128, C_N), np.float32)
    c[:, C_ID:C_ID + 128] = np.eye(128, dtype=np.float32)
    s = np.arange(128)
    c[:, C_TRI:C_TRI + 128] = (s[:, None] <= s[None, :]).astype(np.float32)
    idx = np.arange(64)
    t_, b_ = idx // 16, idx % 16
    c[0:64, C_M64:C_M64 + 64] = ((b_[:, None] == b_[None, :]) & (t_[:, None] <= t_[None, :])).astype(np.float32)
    c[0:4, C_E4:C_E4 + 64] = (np.arange(4)[:, None] == t_[None, :]).astype(np.float32)
    wins = np.array([[2, 4], [8, 16]], np.float32)
    for j in range(2):
        w = np.where(np.arange(128) < 64, wins[j, 0], wins[j, 1]).astype(np.float32)
        c[:, C_INVW + j] = 1.0 / w
        for t in range(16):
            c[:, C_INVC + 16 * j + t] = 1.0 / np.minimum(t + 1.0, w)
            c[:, C_CT + 16 * j + t] = np.where(t < w, 1.0 / w, 0.0) - (1.0 if t == 0 else 0.0)
    return c


_NC_CACHE = {}


def kernel(x_prompt, x_sample, state_pool, state_conv, state_ffn_conv,
           norm1, w_in, pool_w, pool_scale, sgu_w, sgu_b, conv_w, conv_b,
           cnorm_g, cnorm_b, w_out, norm2, w_up, ffn_conv_w, ffn_conv_b,
           w_down, final_norm):
    f = lambda a: np.ascontiguousarray(np.asarray(a, dtype=np.float32))
    if "nc" not in _NC_CACHE:
        _NC_CACHE["nc"] = build_nc()
    nc = _NC_CACHE["nc"]
    shared = dict(norm1=f(norm1), w_in=f(w_in), pool_w=f(pool_w), pool_scale=f(pool_scale), sgu_w=f(sgu_w),
                  sgu_b=f(sgu_b), conv_w=f(conv_w), conv_b=f(conv_b), cnorm_g=f(cnorm_g), cnorm_b=f(cnorm_b),
                  w_out=f(w_out), norm2=f(norm2), w_up=f(w_up), ffn_conv_w=f(ffn_conv_w), ffn_conv_b=f(ffn_conv_b),
                  w_down=f(w_down), final_norm=f(final_norm), cst=_consts())
    x_prompt, x_sample = f(x_prompt), f(x_sample)
    state_pool, state_conv, state_ffn_conv = f(state_pool), f(state_conv), f(state_ffn_conv)
    in_maps = []
    for c in range(8):
        m = dict(shared)
        m["xp"] = x_prompt[c]
        m["xs"] = x_sample[16 * c:16 * c + 16]
        m["st_pool"] = np.ascontiguousarray(state_pool[:, 16 * c:16 * c + 16])
        m["st_conv"] = np.ascontiguousarray(state_conv[:, 16 * c:16 * c + 16])
        m["st_ffn"] = np.ascontiguousarray(state_ffn_conv[:, 16 * c:16 * c + 16])
        in_maps.append(m)
    res = run_bass_kernel_spmd(nc, in_maps, core_ids=list(range(8)))
    R = res.results
    cat0 = lambda k: np.stack([R[c][k] for c in range(8)], axis=0)
    y_p = cat0("y_p")
    y_s = np.concatenate([R[c]["y_s"] for c in range(8)], axis=0)
    pool_p = np.stack([R[c]["o_pool_p"] for c in range(8)], axis=1)
    pool_s = np.concatenate([R[c]["o_pool_s"] for c in range(8)], axis=1)
    conv_p = np.stack([R[c]["o_conv_p"] for c in range(8)], axis=1)
    conv_s = np.concatenate([R[c]["o_conv_s"] for c in range(8)], axis=1)
    ffn_p = np.stack([R[c]["o_ffn_p"] for c in range(8)], axis=1)
    ffn_s = np.concatenate([R[c]["o_ffn_s"] for c in range(8)], axis=1)
    v_s = np.concatenate([R[c]["o_v_s"] for c in range(8)], axis=1)
    return (y_p, y_s, pool_p, pool_s, conv_p, conv_s, ffn_p, ffn_s, v_s)
```
